# Optimizing a Trainium2 kernel written in Bass

```python
import jax, jax.numpy as jnp
from jax import lax
import numpy as np

D_MODEL = 1024
BATCH = 8
SEQ = 4096
DEPTH = 4

GRID_W = 64
CTX_LEN = 256
N_MIXERS = 2
N_CONV_LAYERS = (DEPTH + N_MIXERS - 1) // N_MIXERS
N_ATTN_LAYERS = DEPTH // N_MIXERS
HEAD_DIM = 128
N_HEADS = D_MODEL // HEAD_DIM
N_KV_HEADS = N_HEADS // 4
GQA_GROUP = N_HEADS // N_KV_HEADS
QKV_DIM = (N_HEADS + 2 * N_KV_HEADS) * HEAD_DIM
ROPE_THETA = 10000.0
ROPE_FREQS = HEAD_DIM // 4
Q_BLOCK = 128
D_FF = 2816
CONV_WIDTH = 3
N_SUBLAYERS = 3
N_MOD = 3
NORM_EPS = 1e-6
MACARON_WEIGHT = 0.5

kernel_name = "hybrid_conv_gqa_macaron_dit"


def rms_norm(x, g):
    xf = x.astype(jnp.float32)
    y = xf * lax.rsqrt(jnp.mean(xf * xf, axis=-1, keepdims=True) + NORM_EPS)
    return (y * g.astype(jnp.float32)).astype(x.dtype)


def norm_modulate(x, g, shift, scale):
    return rms_norm(x, g) * (1 + scale) + shift


def swiglu(h, w_in, w_out):
    gate, up = jnp.split(h @ w_in, 2, axis=-1)
    return (jax.nn.silu(gate) * up) @ w_out


def short_conv_mixer(h, w_in, w_conv, w_out):
    b_gate, c_gate, v = jnp.split(h @ w_in, 3, axis=-1)
    u = c_gate * v
    n = u.shape[1]
    half = CONV_WIDTH // 2
    u_pad = jnp.pad(u, ((0, 0), (half, half), (0, 0)))
    y = w_conv[0] * u_pad[:, 0:n]
    for k in range(1, CONV_WIDTH):
        y = y + w_conv[k] * u_pad[:, k:k + n]
    return (b_gate * y) @ w_out


def grid_rope_tables(n, dtype):
    rows_count = n // GRID_W
    row = jnp.broadcast_to(jnp.arange(rows_count, dtype=jnp.int32)[:, None], (rows_count, GRID_W)).reshape(-1)
    col = jnp.broadcast_to(jnp.arange(GRID_W, dtype=jnp.int32)[None, :], (rows_count, GRID_W)).reshape(-1)
    inv_freq = ROPE_THETA ** (-jnp.arange(ROPE_FREQS, dtype=jnp.float32) / ROPE_FREQS)
    ang = jnp.stack([row.astype(jnp.float32)[:, None] * inv_freq,
                     col.astype(jnp.float32)[:, None] * inv_freq], axis=1)
    ang = ang[:, None]
    return jnp.cos(ang).astype(dtype), jnp.sin(ang).astype(dtype)


def apply_rope_2d(x, cos, sin):
    b, n, h, d = x.shape
    xr = x.reshape(b, n, h, 2, 2, ROPE_FREQS)
    x1, x2 = xr[..., 0, :], xr[..., 1, :]
    out = jnp.stack([x1 * cos - x2 * sin, x2 * cos + x1 * sin], axis=-2)
    return out.reshape(b, n, h, d)


def project_qkv(h, w_qkv, q_g, k_g):
    b, n, _ = h.shape
    qkv = h @ w_qkv
    q = qkv[..., :N_HEADS * HEAD_DIM].reshape(b, n, N_HEADS, HEAD_DIM)
    k = qkv[..., N_HEADS * HEAD_DIM:(N_HEADS + N_KV_HEADS) * HEAD_DIM].reshape(b, n, N_KV_HEADS, HEAD_DIM)
    v = qkv[..., (N_HEADS + N_KV_HEADS) * HEAD_DIM:].reshape(b, n, N_KV_HEADS, HEAD_DIM)
    return rms_norm(q, q_g), rms_norm(k, k_g), v


def gqa_block(q, k, v):
    s = jnp.einsum('bqkgd,bskd->bkgqs', q, k).astype(jnp.float32) * (HEAD_DIM ** -0.5)
    p = jax.nn.softmax(s, axis=-1).astype(v.dtype)
    return jnp.einsum('bkgqs,bskd->bqkgd', p, v)


def attention_mixer(h_lat, h_ctx, w_qkv, q_g, k_g, w_o, cos, sin, need_ctx_out):
    b, n, _ = h_lat.shape
    q_l, k_l, v_l = project_qkv(h_lat, w_qkv, q_g, k_g)
    q_c, k_c, v_c = project_qkv(h_ctx, w_qkv, q_g, k_g)
    q_l = apply_rope_2d(q_l, cos, sin)
    k_l = apply_rope_2d(k_l, cos, sin)
    k_all = jnp.concatenate([k_c, k_l], axis=1)
    v_all = jnp.concatenate([v_c, v_l], axis=1)
    n_blocks = n // Q_BLOCK
    qb = q_l.reshape(b, n_blocks, Q_BLOCK, N_KV_HEADS, GQA_GROUP, HEAD_DIM).transpose(1, 0, 2, 3, 4, 5)
    o_l = lax.map(lambda q_blk: gqa_block(q_blk, k_all, v_all), qb)
    o_l = o_l.transpose(1, 0, 2, 3, 4, 5).reshape(b, n, N_HEADS * HEAD_DIM)
    y_lat = o_l @ w_o
    y_ctx = None
    if need_ctx_out:
        o_c = gqa_block(q_c.reshape(b, q_c.shape[1], N_KV_HEADS, GQA_GROUP, HEAD_DIM), k_c, v_c)
        y_ctx = o_c.reshape(b, q_c.shape[1], N_HEADS * HEAD_DIM) @ w_o
    return y_lat, y_ctx


def setup_inputs(seed: int = 0) -> dict:
    key = jax.random.key(seed)
    ks = jax.random.split(key, 20)
    d, f = D_MODEL, D_FF
    nrm = jax.random.normal
    return {
        "x": nrm(ks[0], (BATCH, SEQ, d), jnp.float32),
        "c": nrm(ks[1], (BATCH, d), jnp.float32),
        "ctx": nrm(ks[2], (BATCH, CTX_LEN, d), jnp.float32),
        "c_ctx": nrm(ks[3], (d,), jnp.float32),
        "ada_w": nrm(ks[4], (DEPTH, d, N_SUBLAYERS * N_MOD * d), jnp.float32) * (0.5 * d ** -0.5),
        "ada_b": nrm(ks[5], (DEPTH, N_SUBLAYERS * N_MOD * d), jnp.float32) * 0.02,
        "norm_g": 1.0 + 0.02 * nrm(ks[6], (DEPTH, N_SUBLAYERS, d), jnp.float32),
        "final_g": 1.0 + 0.02 * nrm(ks[7], (d,), jnp.float32),
        "ffn_w_in": nrm(ks[8], (DEPTH, 2, d, 2 * f), jnp.float32) * d ** -0.5,
        "ffn_w_out": nrm(ks[9], (DEPTH, 2, f, d), jnp.float32) * f ** -0.5,
        "conv_w_in": nrm(ks[10], (N_CONV_LAYERS, d, 3 * d), jnp.float32) * d ** -0.5,
        "conv_w": nrm(ks[11], (N_CONV_LAYERS, CONV_WIDTH, d), jnp.float32) * CONV_WIDTH ** -0.5,
        "conv_w_out": nrm(ks[12], (N_CONV_LAYERS, d, d), jnp.float32) * d ** -0.5,
        "attn_w_qkv": nrm(ks[13], (N_ATTN_LAYERS, d, QKV_DIM), jnp.float32) * d ** -0.5,
        "attn_q_g": 1.0 + 0.02 * nrm(ks[14], (N_ATTN_LAYERS, HEAD_DIM), jnp.float32),
        "attn_k_g": 1.0 + 0.02 * nrm(ks[15], (N_ATTN_LAYERS, HEAD_DIM), jnp.float32),
        "attn_w_o": nrm(ks[16], (N_ATTN_LAYERS, N_HEADS * HEAD_DIM, d), jnp.float32) * (N_HEADS * HEAD_DIM) ** -0.5,
    }


def reference(x, c, ctx, c_ctx, ada_w, ada_b, norm_g, final_g, ffn_w_in, ffn_w_out,
              conv_w_in, conv_w, conv_w_out, attn_w_qkv, attn_q_g, attn_k_g, attn_w_o):
    b, n, d = x.shape
    cos, sin = grid_rope_tables(n, x.dtype)
    silu_c = jax.nn.silu(c)
    silu_cc = jax.nn.silu(c_ctx)
    x_lat, x_ctx = x, ctx
    for i in range(DEPTH):
        is_attn = (i % N_MIXERS) == 1
        last = i == DEPTH - 1
        ctx_feeds_mixer = (not last) or is_attn
        m_lat = (silu_c @ ada_w[i] + ada_b[i]).reshape(b, N_SUBLAYERS, N_MOD, 1, d)
        m_ctx = (silu_cc @ ada_w[i] + ada_b[i]).reshape(N_SUBLAYERS, N_MOD, 1, d)
        h = norm_modulate(x_lat, norm_g[i, 0], m_lat[:, 0, 0], m_lat[:, 0, 1])
        x_lat = x_lat + MACARON_WEIGHT * m_lat[:, 0, 2] * swiglu(h, ffn_w_in[i, 0], ffn_w_out[i, 0])
        if ctx_feeds_mixer:
            hc = norm_modulate(x_ctx, norm_g[i, 0], m_ctx[0, 0], m_ctx[0, 1])
            x_ctx = x_ctx + MACARON_WEIGHT * m_ctx[0, 2] * swiglu(hc, ffn_w_in[i, 0], ffn_w_out[i, 0])
        h = norm_modulate(x_lat, norm_g[i, 1], m_lat[:, 1, 0], m_lat[:, 1, 1])
        if is_attn:
            j = i // N_MIXERS
            hc = norm_modulate(x_ctx, norm_g[i, 1], m_ctx[1, 0], m_ctx[1, 1])
            y_lat, y_ctx = attention_mixer(h, hc, attn_w_qkv[j], attn_q_g[j], attn_k_g[j], attn_w_o[j],
                                           cos, sin, not last)
        else:
            j = i // N_MIXERS
            y_lat = short_conv_mixer(h, conv_w_in[j], conv_w[j], conv_w_out[j])
            y_ctx = None
            if not last:
                hc = norm_modulate(x_ctx, norm_g[i, 1], m_ctx[1, 0], m_ctx[1, 1])
                y_ctx = short_conv_mixer(hc, conv_w_in[j], conv_w[j], conv_w_out[j])
        x_lat = x_lat + m_lat[:, 1, 2] * y_lat
        if not last:
            x_ctx = x_ctx + m_ctx[1, 2] * y_ctx
        h = norm_modulate(x_lat, norm_g[i, 2], m_lat[:, 2, 0], m_lat[:, 2, 1])
        x_lat = x_lat + MACARON_WEIGHT * m_lat[:, 2, 2] * swiglu(h, ffn_w_in[i, 1], ffn_w_out[i, 1])
        if not last:
            hc = norm_modulate(x_ctx, norm_g[i, 2], m_ctx[2, 0], m_ctx[2, 1])
            x_ctx = x_ctx + MACARON_WEIGHT * m_ctx[2, 2] * swiglu(hc, ffn_w_in[i, 1], ffn_w_out[i, 1])
    return rms_norm(x_lat, final_g)
```

```python
import numpy as np
from contextlib import ExitStack
import concourse.bass as bass
import concourse.mybir as mybir
from concourse.bass_utils import run_bass_kernel_spmd

F32 = mybir.dt.float32
BF16 = mybir.dt.bfloat16
AF = mybir.ActivationFunctionType
ALU = mybir.AluOpType

D = 1024
KC = 8
DFF = 2816
FC = 22
SEQ = 4096
CTX = 256
NTOK = SEQ + CTX
DEPTH = 4
HD = 128
NH = 8
NKV = 2
GRID_W = 64
EPS = 1e-6
NCORES = 8

C_ONES = 0
C_ONESM = 256
C_ONESH = 512
C_PERM = 768
C_NEGH = 1280
C_SMALL = 3328
NSMALL = 460
C_SCC = 5168
C_MODS = 5248
C_A = 7552
C_GH = 8320
C_ZERO = 8832
ARENA = 10240
R1 = ARENA
R1_SZ = 90112
R2 = ARENA + R1_SZ
R2_SZ = 45056
XB = R2 + R2_SZ
HB = XB + 16384
ACTB = HB + 8192
SQ = ACTB + 22528
RSTD = SQ + 2048
XR = RSTD + 2048
SL = XR + 4096
E2B = SL + 4096
SB_END = E2B + 3 * 2048
assert SB_END <= 212480
GR = 512

SAME_ENGINE_SYNC = True
SEM_ROLL = 30000


class Buf:
    __slots__ = ("ap", "keys")

    def __init__(self, ap, keys):
        self.ap = ap
        self.keys = keys


def _keys(lst):
    out = []
    for x in lst:
        if isinstance(x, Buf):
            out.extend(x.keys)
        else:
            out.append(x)
    return out


class Trk:
    def __init__(self, nc, es):
        self.nc = nc
        self.es = es
        self.eng = {"pe": nc.tensor, "act": nc.scalar, "dve": nc.vector, "pool": nc.gpsimd, "sp": nc.sync}
        self.esem = {}
        self.ecnt = {}
        self.pe_sems = set()
        self.waited = {e: {} for e in self.eng}
        self.state = {}
        self.dsem = {}
        self.nsem = 0
        self.ninst = 0
        for e in ("pe", "act", "dve", "pool"):
            self._roll(e)

    def _newsem(self, name):
        self.nsem += 1
        return self.es.enter_context(self.nc.semaphore(f"{name}{self.nsem}"))

    def _roll(self, e):
        self.esem[e] = self._newsem(e)
        self.ecnt[e] = 0
        if e == "pe":
            self.pe_sems.add(id(self.esem[e]))

    def _deps(self, rk, wk):
        deps = {}
        st_ = self.state
        for k in rk:
            st = st_.get(k)
            if st is not None and st[0] is not None:
                ts = st[0]
                i = id(ts[0])
                d = deps.get(i)
                if d is None or d[1] < ts[1]:
                    deps[i] = ts
        for k in wk:
            st = st_.get(k)
            if st is not None:
                ts = st[0]
                if ts is not None:
                    i = id(ts[0])
                    d = deps.get(i)
                    if d is None or d[1] < ts[1]:
                        deps[i] = ts
                for i, ts in st[1].items():
                    d = deps.get(i)
                    if d is None or d[1] < ts[1]:
                        deps[i] = ts
        return deps

    def _wait(self, e, deps):
        w = self.waited[e]
        eng = self.eng[e]
        for i, (sem, val) in deps.items():
            if e == "pe" and i in self.pe_sems:
                continue
            if (not SAME_ENGINE_SYNC) and e in self.esem and sem is self.esem[e]:
                continue
            if w.get(i, 0) < val:
                eng.wait_ge(sem, val)
                w[i] = val

    def _commit(self, ts, rk, wk):
        st_ = self.state
        i = id(ts[0])
        for k in wk:
            st_[k] = [ts, {}]
        for k in rk:
            st = st_.get(k)
            if st is None:
                st = st_[k] = [None, {}]
            st[1][i] = ts

    def op(self, e, fn, reads=(), writes=()):
        rk = _keys(reads)
        wk = _keys(writes)
        self._wait(e, self._deps(rk, wk))
        ins = fn(self.eng[e])
        if self.ecnt[e] >= SEM_ROLL:
            self._roll(e)
        self.ecnt[e] += 1
        ins.then_inc(self.esem[e], 1)
        self._commit((self.esem[e], self.ecnt[e]), rk, wk)
        self.ninst += 1

    def dma(self, e, out, in_, reads, writes, semkey, **kw):
        rk = _keys(reads)
        wk = _keys(writes)
        deps = self._deps(rk, wk)
        ds = self.dsem.get(semkey)
        if ds is None:
            ds = self.dsem[semkey] = [self._newsem("d"), 0]
        if ds[1] > 0:
            deps[id(ds[0])] = (ds[0], ds[1])
        self._wait(e, deps)
        self.eng[e].dma_start(out=out, in_=in_, **kw).then_inc(ds[0], 16)
        ds[1] += 16
        self._commit((ds[0], ds[1]), rk, wk)
        self.ninst += 1

    def finish(self):
        sp = self.eng["sp"]
        for ds in self.dsem.values():
            if ds[1] > 0:
                sp.wait_ge(ds[0], ds[1])


class Tile:
    def __init__(self, tid, kind, g0, T):
        self.tid = tid
        self.kind = kind
        self.g0 = g0
        self.T = T
        self.k = 0 if kind == "lat" else 1


CTX_TILE = Tile(0, "ctx", 0, CTX)
LAT_TILES = [Tile(1 + i, "lat", CTX + 512 * i, 512) for i in range(8)]


class Job:
    def __init__(self, sub, tile, first, last):
        self.sub = sub
        self.tile = tile
        self.first = first
        self.last = last
        self.par = 0


def build_program(n_sub=12):
    nc = bass.Bass("TRN2", target_bir_lowering=False)
    es = ExitStack()

    def din(name, shape, dt=F32):
        return nc.dram_tensor(name, list(shape), dt, kind="ExternalInput").ap()

    xT = din("xT", [KC, 128, SEQ])
    ctxT = din("ctxT", [KC, 128, CTX])
    smallp = din("smallp", [128, NSMALL])
    permd = din("perm", [128, 128])
    cosd = din("cosT", [128, SEQ])
    sind = din("sinT", [128, SEQ])
    adaw = din("ada_w", [DEPTH, 18, 128, KC * 512])
    fwin = din("ffn_w_in", [DEPTH, 2, FC, 128, KC * 256])
    fwout = din("ffn_w_out", [DEPTH, 2, FC, 128, D])
    cwin = din("conv_w_in", [2, 128, KC, 3 * D])
    cwout = din("conv_w_out", [2, 128, KC, D])
    aqkv = din("attn_w_qkv", [2, 128, KC, 1536])
    awo = din("attn_w_o", [2, 128, KC, D])
    outT = nc.dram_tensor("outT", [KC, 128, SEQ], F32, kind="ExternalOutput").ap()
    Xd = nc.dram_tensor("Xs", [KC, 128, NTOK], F32, kind="Internal").ap()
    UW = NTOK + 4
    Ud = nc.dram_tensor("Us", [KC, 128, UW], F32, kind="Internal").ap()
    QTd = nc.dram_tensor("QTs", [128, NH, NTOK], BF16, kind="Internal").ap()

    AR = es.enter_context(nc.sbuf_tensor("AR", [128, SB_END // 4], F32))
    PSt = es.enter_context(nc.psum_tensor("PS", [128, 8, 512], F32))
    tr = Trk(nc, es)
    op = tr.op
    dma = tr.dma

    def sview(off, dims, dt):
        n = int(np.prod(dims))
        esz = 4 if dt is F32 else 2
        nb = n * esz
        assert off % 4 == 0 and nb % 4 == 0 and off + nb <= SB_END, (off, dims)
        ap = AR[:, off // 4:(off + nb) // 4]
        if dt is BF16:
            ap = ap.bitcast(BF16)
        if len(dims) == 2:
            ap = ap.rearrange("p (a b) -> p a b", b=dims[1])
        elif len(dims) == 3:
            ap = ap.rearrange("p (a b c) -> p a b c", b=dims[1], c=dims[2])
        keys = [("S", g) for g in range(off // GR, (off + nb - 1) // GR + 1)]
        return Buf(ap, keys)

    PS = [Buf(PSt[:, b, :], [("P", b)]) for b in range(8)]

    ONES = sview(C_ONES, [128], BF16)
    ONESM = sview(C_ONESM, [128], BF16)
    ONESH = sview(C_ONESH, [128], BF16)
    PERM = sview(C_PERM, [128], F32)
    NEGH = sview(C_NEGH, [512], F32)
    SMALL = sview(C_SMALL, [NSMALL], F32)
    SCC = sview(C_SCC, [KC, 2], F32)
    MODS = sview(C_MODS, [8, 72], F32)
    AV = sview(C_A, [24, 8], F32)
    GH = sview(C_GH, [16, 8], F32)
    ZERO = sview(C_ZERO, [8, 1], F32)

    XBf = sview(XB, [KC, 512], F32)
    XBc = [sview(XB + kc * 2048, [512], F32) for kc in range(KC)]
    HBf = sview(HB, [KC, 512], BF16)
    HBc = [sview(HB + kc * 1024, [512], BF16) for kc in range(KC)]
    ACTc = [sview(ACTB + fc * 1024, [512], BF16) for fc in range(FC)]
    SQs = [sview(SQ + i * 1024, [512], BF16) for i in range(2)]
    RSTDb = sview(RSTD, [512], F32)
    XRs = [sview(XR + i * 2048, [512], F32) for i in range(2)]
    SLs = [sview(SL + i * 2048, [512], F32) for i in range(2)]
    E2s = [sview(E2B + i * 2048, [512], F32) for i in range(3)]
    e2ctr = [0]

    O_CC, O_ADAB, O_NG, O_FG, O_CW, O_QKG = 0, 16, 304, 400, 408, 456

    def smallcol(o, n=1):
        return SMALL.ap[:, o:o + n]

    op("pool", lambda e: e.memset(ONES.ap, 1.0), writes=[ONES])
    op("pool", lambda e: e.memset(ONESM.ap, 1.0 / 1024.0), writes=[ONESM])
    op("pool", lambda e: e.memset(ONESH.ap, 1.0 / 128.0), writes=[ONESH])
    op("pool", lambda e: e.memset(NEGH.ap, -0.5), writes=[NEGH])
    op("pool", lambda e: e.memset(ZERO.ap, 0.0), writes=[ZERO])
    dma("sp", SMALL.ap, smallp, [], [SMALL], "small")
    dma("sp", PERM.ap, permd, [], [PERM], "perm")
    for col in (0, CTX + 1, CTX + 2, UW - 1):
        dma("sp", Ud[:, :, col:col + 1].rearrange("k p o -> p k o"), ZERO.ap, [ZERO], [("Upad", col)], "upad",
            allow_slow_non_contiguous=True)
    op("act", lambda e: e.activation(out=SCC.ap.rearrange("p k t -> p (k t)"), in_=smallcol(O_CC, 16), func=AF.Silu),
       reads=[SMALL], writes=[SCC])

    STG = [sview(R1 + i * 16384, [KC, 512], F32) for i in range(2)]
    nst = 0
    for i in range(DEPTH):
        for jb in range(18):
            stg = STG[nst % 2]
            dma("sp", stg.ap.rearrange("p k n -> p (k n)"), adaw[i, jb], [], [stg], ("ada", nst % 2))
            nst += 1
            for j4 in range(4):
                j = jb * 4 + j4
                for kc in range(KC):
                    op("pe", lambda e, j=j, j4=j4, kc=kc, stg=stg: e.matmul(
                        PSt[:, 7, 2 * j:2 * j + 2], lhsT=stg.ap[:, kc, j4 * 128:(j4 + 1) * 128],
                        rhs=SCC.ap[:, kc, :], start=(kc == 0), stop=(kc == KC - 1)),
                       reads=[stg, SCC], writes=[PS[7]])
        psv = PSt[:, 7, 0:144].rearrange("p (j t) -> p j t", t=2)
        for kind in range(2):
            op("dve", lambda e, i=i, kind=kind, psv=psv: e.tensor_tensor(
                out=MODS.ap[:, i * 2 + kind, :], in0=psv[:, :, kind], in1=smallcol(O_ADAB + i * 72, 72), op=ALU.add),
               reads=[PS[7], SMALL], writes=[MODS])
        for s in range(3):
            for kind in range(2):
                m = MODS.ap[:, i * 2 + kind, :]
                op("dve", lambda e, m=m, i=i, s=s, kind=kind: e.scalar_tensor_tensor(
                    out=AV.ap[:, (i * 3 + s) * 2 + kind, :], in0=m[:, (s * 3 + 1) * 8:(s * 3 + 1) * 8 + 8], scalar=1.0,
                    in1=smallcol(O_NG + (i * 3 + s) * 8, 8), op0=ALU.add, op1=ALU.mult),
                   reads=[MODS, SMALL], writes=[AV])
                if s != 1:
                    jj = 0 if s == 0 else 1
                    op("dve", lambda e, m=m, i=i, s=s, jj=jj, kind=kind: e.tensor_scalar(
                        out=GH.ap[:, (i * 2 + jj) * 2 + kind, :], in0=m[:, (s * 3 + 2) * 8:(s * 3 + 2) * 8 + 8],
                        scalar1=0.5, scalar2=None, op0=ALU.mult),
                       reads=[MODS], writes=[GH])

    def mod_A(i, s, kind):
        return AV.ap[:, (i * 3 + s) * 2 + kind, :]

    def mod_S(i, s, kind):
        return MODS.ap[:, i * 2 + kind, (s * 3) * 8:(s * 3) * 8 + 8]

    def mod_G(i, s, kind):
        if s == 1:
            return MODS.ap[:, i * 2 + kind, (s * 3 + 2) * 8:(s * 3 + 2) * 8 + 8]
        return GH.ap[:, (i * 2 + (0 if s == 0 else 1)) * 2 + kind, :]

    class Sub:
        pass

    subs = []
    for i in range(DEPTH):
        is_attn = (i % 2) == 1
        last = i == DEPTH - 1
        for s in range(3):
            sb = Sub()
            sb.layer, sb.s = i, s
            if s == 0:
                sb.types = ["ffn"]
                sb.ctx = True
            elif s == 2:
                sb.types = ["ffn"]
                sb.ctx = not last
            elif is_attn:
                sb.types = ["attnA", "attnB"]
                sb.ctx = True
                sb.ctx_out = not last
            else:
                sb.types = ["convA", "convB"]
                sb.ctx = not last
            subs.append(sb)
    subs = subs[:n_sub]

    class Phase:
        pass

    phases = []
    for si, sb in enumerate(subs):
        for ty in sb.types:
            ph = Phase()
            ph.ty = ty
            ph.sub = sb
            ph.layer, ph.s = sb.layer, sb.s
            ph.first_sub = (si == 0)
            use_ctx = sb.ctx
            if ty == "attnB":
                use_ctx = sb.ctx_out
            ph.tiles = ([CTX_TILE] if use_ctx else []) + LAT_TILES
            phases.append(ph)
    phf = Phase()
    phf.ty = "final"
    phf.layer, phf.s = 0, 0
    phf.first_sub = (len(subs) == 0)
    phf.tiles = LAT_TILES
    phases.append(phf)

    jobs = []
    for ph in phases:
        n = len(ph.tiles)
        for ti, tl in enumerate(ph.tiles):
            jb = Job(ph, tl, ti == 0, ti == n - 1)
            jb.par = len(jobs) % 2
            jobs.append(jb)

    def src_full(ph, tl):
        if ph.first_sub and ph.ty in ("ffn", "final"):
            if tl.kind == "lat":
                return xT[:, :, tl.g0 - CTX:tl.g0 - CTX + tl.T].rearrange("k p t -> p k t"), []
            return ctxT[:, :, 0:tl.T].rearrange("k p t -> p k t"), []
        return Xd[:, :, tl.g0:tl.g0 + tl.T].rearrange("k p t -> p k t"), [("X", tl.tid, dc) for dc in range(KC)]

    def src_chunk(ph, tl, dc):
        if ph.first_sub and ph.ty == "ffn":
            if tl.kind == "lat":
                return xT[dc, :, tl.g0 - CTX:tl.g0 - CTX + tl.T], []
            return ctxT[dc, :, 0:tl.T], []
        return Xd[dc, :, tl.g0:tl.g0 + tl.T], [("X", tl.tid, dc)]

    def dst_chunk(tl, dc):
        return Xd[dc, :, tl.g0:tl.g0 + tl.T], [("X", tl.tid, dc)]

    def ucol(tl):
        return (1 + tl.g0) if tl.kind == "ctx" else (3 + tl.g0)

    def wdma(out_ap, in_ap, buf, idx):
        dma("pool", out_ap, in_ap, [], [buf], ("w", idx))

    def setup_weights(ph):
        r1, r2 = [], []
        ty = ph.ty
        if ty == "ffn":
            i, j = ph.layer, (0 if ph.s == 0 else 1)
            ph.Win = [sview(R1 + fc * 4096, [KC, 256], BF16) for fc in range(FC)]
            ph.Wout = [sview(R2 + fp * 4096, [2, D], BF16) for fp in range(FC // 2)]
            for fc in range(FC):
                r1.append(lambda fc=fc: wdma(ph.Win[fc].ap.rearrange("p k n -> p (k n)"), fwin[i, j, fc], ph.Win[fc], fc))
            for fp in range(FC // 2):
                r2.append(lambda fp=fp: wdma(ph.Wout[fp].ap, fwout[i, j, 2 * fp:2 * fp + 2].rearrange("f p n -> p f n"),
                                            ph.Wout[fp], FC + fp))
        elif ty == "convA":
            jj = ph.layer // 2
            ph.Wp = [sview(R1 + pi * 8192, [KC, 512], BF16) for pi in range(4)]
            for pi in range(4):
                r1.append(lambda pi=pi: wdma(ph.Wp[pi].ap, cwin[jj, :, :, D + pi * 512:D + (pi + 1) * 512], ph.Wp[pi], pi))
        elif ty == "convB":
            jj = ph.layer // 2
            ph.Wp = [sview(R1 + pi * 8192, [KC, 512], BF16) for pi in range(2)]
            ph.Wo = [sview(R2 + pi * 4096, [2, D], BF16) for pi in range(4)]
            for pi in range(2):
                r1.append(lambda pi=pi: wdma(ph.Wp[pi].ap, cwin[jj, :, :, pi * 512:(pi + 1) * 512], ph.Wp[pi], pi))
            for pi in range(4):
                r2.append(lambda pi=pi: wdma(ph.Wo[pi].ap, cwout[jj, :, 2 * pi:2 * pi + 2, :], ph.Wo[pi], FC + pi))
        elif ty == "attnA":
            jj = ph.layer // 2
            ph.Wp = [sview(R1 + pi * 8192, [KC, 512], BF16) for pi in range(3)]
            for pi in range(3):
                r1.append(lambda pi=pi: wdma(ph.Wp[pi].ap, aqkv[jj, :, :, pi * 512:(pi + 1) * 512], ph.Wp[pi], pi))
        elif ty == "attnB":
            jj = ph.layer // 2
            ph.Wo = [sview(R2 + pi * 4096, [2, D], BF16) for pi in range(4)]
            for pi in range(4):
                r2.append(lambda pi=pi: wdma(ph.Wo[pi].ap, awo[jj, :, 2 * pi:2 * pi + 2, :], ph.Wo[pi], FC + pi))
        return r1, r2

    KT = sview(R1 + 24576, [NKV, NTOK], BF16)
    VV = sview(R1 + 24576 + 17408, [NTOK // 128, 256], BF16)
    PBs = [sview(R1 + 59392 + i * 1024, [512], BF16) for i in range(4)]
    OBUF = sview(R2 + 16384, [NH, 512], BF16)
    OBc = [sview(R2 + 16384 + h * 1024, [512], BF16) for h in range(NH)]
    RDEN = sview(R2 + 24576, [512], F32)
    QST = sview(ACTB, [NH, 512], BF16)
    QRAW = [sview(ACTB + 8192 + i * 2048, [512], F32) for i in range(2)]
    QN = [sview(ACTB + 12288 + i * 2048, [512], F32) for i in range(2)]
    COSb = sview(ACTB + 16384, [512], F32)
    SINb = sview(ACTB + 18432, [512], F32)
    RQ = sview(ACTB + 20480, [512], F32)
    UBs = [sview(R1 + 32768 + i * 16448, [KC, 514], F32) for i in range(2)]
    CT = [sview(R1 + 65664 + i * 2048, [512], F32) for i in range(2)]
    CSL = [sview(R1 + 69760 + i * 2048, [512], F32) for i in range(2)]
    ZB = [sview(R2 + 16384 + kc * 1024, [512], BF16) for kc in range(KC)]

    def emit_load(job):
        ph, tl = job.sub, job.tile
        T = tl.T
        if ph.ty == "attnB":
            dma("sp", HBf.ap[:, :, :T], QTd[:, :, tl.g0:tl.g0 + T], [("QT", tl.tid)], [HBf], "qtl")
            return
        ap, keys = src_full(ph, tl)
        dma("sp", XBf.ap[:, :, :T], ap, keys, [XBf], "xl")
        if ph.ty == "convB":
            ub = UBs[job.par]
            c0 = ucol(tl) - 1
            rk = [("U", t2.tid, dc) for t2 in ph.tiles if t2.kind == tl.kind and abs(t2.tid - tl.tid) <= 1 for dc in range(KC)]
            rk += [("Upad", c) for c in (0, CTX + 1, CTX + 2, UW - 1)]
            dma("sp", ub.ap[:, :, :T + 2], Ud[:, :, c0:c0 + T + 2].rearrange("k p t -> p k t"), rk, [ub], ("ul", job.par))

    def emit_n1(job):
        ph, tl = job.sub, job.tile
        if ph.ty == "attnB":
            return
        T = tl.T
        for kc in range(KC):
            s = SQs[kc % 2]
            op("act", lambda e, s=s, kc=kc: e.activation(out=s.ap[:, :T], in_=XBc[kc].ap[:, :T], func=AF.Square),
               reads=[XBc[kc]], writes=[s])
            op("pe", lambda e, s=s, kc=kc: e.matmul(PS[6].ap[:, :T], lhsT=ONESM.ap, rhs=s.ap[:, :T],
                                                  start=(kc == 0), stop=(kc == KC - 1)),
               reads=[s, ONESM], writes=[PS[6]])
        op("dve", lambda e: e.tensor_scalar(out=RSTDb.ap[:, :T], in0=PS[6].ap[:, :T], scalar1=EPS, scalar2=None, op0=ALU.add),
           reads=[PS[6]], writes=[RSTDb])
        op("pool", lambda e: e.tensor_tensor(out=RSTDb.ap[:, :T], in0=RSTDb.ap[:, :T], in1=NEGH.ap[:, :T], op=ALU.pow),
           reads=[RSTDb, NEGH], writes=[RSTDb])

    def emit_n2(job):
        ph, tl = job.sub, job.tile
        if ph.ty in ("attnB", "final"):
            return
        T = tl.T
        A = mod_A(ph.layer, ph.s, tl.k)
        S = mod_S(ph.layer, ph.s, tl.k)
        for kc in range(KC):
            xr = XRs[kc % 2]
            op("dve", lambda e, xr=xr, kc=kc: e.tensor_tensor(out=xr.ap[:, :T], in0=XBc[kc].ap[:, :T], in1=RSTDb.ap[:, :T],
                                                            op=ALU.mult),
               reads=[XBc[kc], RSTDb], writes=[xr])
            op("act", lambda e, xr=xr, kc=kc: e.activation(out=HBc[kc].ap[:, :T], in_=xr.ap[:, :T], func=AF.Identity,
                                                         scale=A[:, kc:kc + 1], bias=S[:, kc:kc + 1]),
               reads=[xr, AV, MODS], writes=[HBc[kc]])

    def e2_reload(job, dc):
        ph, tl = job.sub, job.tile
        slot = E2s[(e2ctr[0] + dc) % 3]
        ap, keys = src_chunk(ph, tl, dc)
        dma("sp", slot.ap[:, :tl.T], ap, keys, [slot], ("e2l", (e2ctr[0] + dc) % 3))

    def e2_update(job, dc, yb):
        ph, tl = job.sub, job.tile
        T = tl.T
        si = (e2ctr[0] + dc) % 3
        slot = E2s[si]
        G = mod_G(ph.layer, ph.s, tl.k)
        op("dve", lambda e: e.scalar_tensor_tensor(out=slot.ap[:, :T], in0=yb.ap[:, :T], scalar=G[:, dc:dc + 1],
                                                   in1=slot.ap[:, :T], op0=ALU.mult, op1=ALU.add),
           reads=[yb, slot, MODS, GH], writes=[slot])
        ap, keys = dst_chunk(tl, dc)
        dma("sp", ap, slot.ap[:, :T], [slot], keys, ("e2s", si))

    def out_proj(job, wfn, rhs_list, nk):
        tl = job.tile
        T = tl.T
        e2_reload(job, 0)
        e2_reload(job, 1)
        for dc in range(KC):
            yb = PS[4 + dc % 2]
            for k in range(nk):
                wb, wap = wfn(k, dc)
                op("pe", lambda e, wap=wap, k=k, yb=yb: e.matmul(yb.ap[:, :T], lhsT=wap, rhs=rhs_list[k].ap[:, :T],
                                                               start=(k == 0), stop=(k == nk - 1)),
                   reads=[wb, rhs_list[k]], writes=[yb])
            e2_update(job, dc, yb)
            if dc + 2 < KC:
                e2_reload(job, dc + 2)
        e2ctr[0] += KC

    def main_ffn(job, hook_mid, hook_p2):
        ph, tl = job.sub, job.tile
        T = tl.T
        for fc in range(FC):
            b = fc % 2
            w = ph.Win[fc]
            for half in range(2):
                psb = PS[2 * half + b]
                for kc in range(KC):
                    op("pe", lambda e, w=w, half=half, kc=kc, psb=psb: e.matmul(
                        psb.ap[:, :T], lhsT=w.ap[:, kc, half * 128:(half + 1) * 128], rhs=HBc[kc].ap[:, :T],
                        start=(kc == 0), stop=(kc == KC - 1)),
                       reads=[w, HBc[kc]], writes=[psb])
            sl = SLs[b]
            op("act", lambda e, sl=sl, b=b: e.activation(out=sl.ap[:, :T], in_=PS[b].ap[:, :T], func=AF.Silu),
               reads=[PS[b]], writes=[sl])
            op("dve", lambda e, sl=sl, b=b, fc=fc: e.tensor_tensor(out=ACTc[fc].ap[:, :T], in0=sl.ap[:, :T],
                                                                 in1=PS[2 + b].ap[:, :T], op=ALU.mult),
               reads=[sl, PS[2 + b]], writes=[ACTc[fc]])
            if fc == 6:
                hook_mid()
        hook_p2()
        out_proj(job, lambda k, dc: (ph.Wout[k // 2], ph.Wout[k // 2].ap[:, k % 2, dc * 128:(dc + 1) * 128]), ACTc, FC)

    def main_final(job, hook_mid, hook_p2):
        tl = job.tile
        T = tl.T
        t0 = tl.g0 - CTX
        for dc in range(KC):
            si = (e2ctr[0] + dc) % 3
            slot = E2s[si]
            op("dve", lambda e, dc=dc, slot=slot: e.scalar_tensor_tensor(
                out=slot.ap[:, :T], in0=XBc[dc].ap[:, :T], scalar=smallcol(O_FG + dc), in1=RSTDb.ap[:, :T],
                op0=ALU.mult, op1=ALU.mult),
               reads=[XBc[dc], SMALL, RSTDb], writes=[slot])
            dma("sp", outT[dc, :, t0:t0 + T], slot.ap[:, :T], [slot], [("OUT", tl.tid, dc)], ("e2s", si))
        e2ctr[0] += KC
        hook_mid()
        hook_p2()

    def main_convA(job, hook_mid, hook_p2):
        ph, tl = job.sub, job.tile
        T = tl.T
        c0 = ucol(tl)
        for c in range(KC):
            b = c % 2
            for half in range(2):
                w = ph.Wp[2 * half + c // 4]
                psb = PS[2 * half + b]
                for kc in range(KC):
                    op("pe", lambda e, w=w, kc=kc, psb=psb, c=c: e.matmul(
                        psb.ap[:, :T], lhsT=w.ap[:, kc, (c % 4) * 128:(c % 4 + 1) * 128], rhs=HBc[kc].ap[:, :T],
                        start=(kc == 0), stop=(kc == KC - 1)),
                       reads=[w, HBc[kc]], writes=[psb])
            cs = CSL[b]
            op("act", lambda e, cs=cs, b=b: e.activation(out=cs.ap[:, :T], in_=PS[b].ap[:, :T], func=AF.Copy),
               reads=[PS[b]], writes=[cs])
            si = (e2ctr[0] + c) % 3
            slot = E2s[si]
            op("dve", lambda e, cs=cs, b=b, slot=slot: e.tensor_tensor(out=slot.ap[:, :T], in0=cs.ap[:, :T],
                                                                     in1=PS[2 + b].ap[:, :T], op=ALU.mult),
               reads=[cs, PS[2 + b]], writes=[slot])
            dma("sp", Ud[c, :, c0:c0 + T], slot.ap[:, :T], [slot], [("U", tl.tid, c)], ("e2s", si))
            if c == 3:
                hook_mid()
        e2ctr[0] += KC
        hook_p2()

    def main_convB(job, hook_mid, hook_p2):
        ph, tl = job.sub, job.tile
        T = tl.T
        ub = UBs[job.par]
        jj = ph.layer // 2
        for c in range(KC):
            b = c % 2
            w = ph.Wp[c // 4]
            for kc in range(KC):
                op("pe", lambda e, w=w, kc=kc, c=c, b=b: e.matmul(
                    PS[b].ap[:, :T], lhsT=w.ap[:, kc, (c % 4) * 128:(c % 4 + 1) * 128], rhs=HBc[kc].ap[:, :T],
                    start=(kc == 0), stop=(kc == KC - 1)),
                   reads=[w, HBc[kc]], writes=[PS[b]])
            t = CT[b]
            cw = [smallcol(O_CW + (jj * 3 + k) * 8 + c) for k in range(3)]
            op("act", lambda e, t=t, c=c, cw=cw: e.activation(out=t.ap[:, :T], in_=ub.ap[:, c, 0:T], func=AF.Identity,
                                                            scale=cw[0]),
               reads=[ub, SMALL], writes=[t])
            for k in (1, 2):
                op("dve", lambda e, t=t, c=c, k=k, cw=cw: e.scalar_tensor_tensor(
                    out=t.ap[:, :T], in0=ub.ap[:, c, k:k + T], scalar=cw[k], in1=t.ap[:, :T], op0=ALU.mult, op1=ALU.add),
                   reads=[ub, SMALL, t], writes=[t])
            op("dve", lambda e, t=t, c=c, b=b: e.tensor_tensor(out=ZB[c].ap[:, :T], in0=t.ap[:, :T], in1=PS[b].ap[:, :T],
                                                             op=ALU.mult),
               reads=[t, PS[b]], writes=[ZB[c]])
            if c == 3:
                hook_mid()
        hook_p2()
        out_proj(job, lambda k, dc: (ph.Wo[k // 2], ph.Wo[k // 2].ap[:, k % 2, dc * 128:(dc + 1) * 128]), ZB, KC)

    def main_attnA(job, hook_mid, hook_p2):
        ph, tl = job.sub, job.tile
        T = tl.T
        lat = tl.kind == "lat"
        jj = ph.layer // 2
        if lat:
            t0 = tl.g0 - CTX
            dma("sp", COSb.ap, cosd[:, t0:t0 + T], [], [COSb], "cos")
            dma("sp", SINb.ap, sind[:, t0:t0 + T], [], [SINb], "sin")
        for j in range(10):
            b = j % 2
            w = ph.Wp[(j * 128) // 512]
            co = (j * 128) % 512
            for kc in range(KC):
                op("pe", lambda e, w=w, kc=kc, co=co, b=b: e.matmul(
                    PS[b].ap[:, :T], lhsT=w.ap[:, kc, co:co + 128], rhs=HBc[kc].ap[:, :T],
                    start=(kc == 0), stop=(kc == KC - 1)),
                   reads=[w, HBc[kc]], writes=[PS[b]])
            qr = QRAW[b]
            sq = SQs[b]
            op("act", lambda e, qr=qr, b=b: e.activation(out=qr.ap[:, :T], in_=PS[b].ap[:, :T], func=AF.Copy),
               reads=[PS[b]], writes=[qr])
            op("act", lambda e, sq=sq, b=b: e.activation(out=sq.ap[:, :T], in_=PS[b].ap[:, :T], func=AF.Square),
               reads=[PS[b]], writes=[sq])
            op("pe", lambda e, sq=sq: e.matmul(PS[7].ap[:, :T], lhsT=ONESH.ap, rhs=sq.ap[:, :T], start=True, stop=True),
               reads=[sq, ONESH], writes=[PS[7]])
            op("dve", lambda e: e.tensor_scalar(out=RQ.ap[:, :T], in0=PS[7].ap[:, :T], scalar1=EPS, scalar2=None, op0=ALU.add),
               reads=[PS[7]], writes=[RQ])
            op("pool", lambda e: e.tensor_tensor(out=RQ.ap[:, :T], in0=RQ.ap[:, :T], in1=NEGH.ap[:, :T], op=ALU.pow),
               reads=[RQ, NEGH], writes=[RQ])
            g = smallcol(O_QKG + jj * 2 + (0 if j < NH else 1))
            qn = QN[b]
            op("dve", lambda e, qr=qr, qn=qn, g=g: e.scalar_tensor_tensor(
                out=qn.ap[:, :T], in0=qr.ap[:, :T], scalar=g, in1=RQ.ap[:, :T], op0=ALU.mult, op1=ALU.mult),
               reads=[qr, RQ, SMALL], writes=[qn])
            if j < NH:
                dest_ap = QST.ap[:, j, :T]
                dest_b = QST
            else:
                dest_ap = KT.ap[:, j - NH, tl.g0:tl.g0 + T]
                dest_b = KT
            if lat:
                pp = PS[4 + b]
                op("pe", lambda e, qn=qn, pp=pp: e.matmul(pp.ap[:, :T], lhsT=PERM.ap, rhs=qn.ap[:, :T], start=True, stop=True),
                   reads=[qn, PERM], writes=[pp])
                t1, t2 = SLs[0], SLs[1]
                op("dve", lambda e, qn=qn, t1=t1: e.tensor_tensor(out=t1.ap[:, :T], in0=qn.ap[:, :T], in1=COSb.ap[:, :T],
                                                                op=ALU.mult),
                   reads=[qn, COSb], writes=[t1])
                op("dve", lambda e, pp=pp, t2=t2: e.tensor_tensor(out=t2.ap[:, :T], in0=pp.ap[:, :T], in1=SINb.ap[:, :T],
                                                                op=ALU.mult),
                   reads=[pp, SINb], writes=[t2])
                op("pool", lambda e, t1=t1, t2=t2, dest_ap=dest_ap: e.tensor_tensor(out=dest_ap, in0=t1.ap[:, :T],
                                                                                   in1=t2.ap[:, :T], op=ALU.add),
                   reads=[t1, t2], writes=[dest_b])
            else:
                op("act", lambda e, qn=qn, dest_ap=dest_ap: e.activation(out=dest_ap, in_=qn.ap[:, :T], func=AF.Copy),
                   reads=[qn], writes=[dest_b])
            if j == 4:
                hook_mid()
        wv = ph.Wp[2]
        for st in range(T // 128):
            pv = PS[2 + st % 2]
            for kc in range(KC):
                op("pe", lambda e, kc=kc, st=st, pv=pv: e.matmul(
                    pv.ap[:, 0:256], lhsT=HBc[kc].ap[:, st * 128:(st + 1) * 128], rhs=wv.ap[:, kc, 256:512],
                    start=(kc == 0), stop=(kc == KC - 1)),
                   reads=[HBc[kc], wv], writes=[pv])
            ci = (tl.g0 + st * 128) // 128
            op("act", lambda e, pv=pv, ci=ci: e.activation(out=VV.ap[:, ci, :], in_=pv.ap[:, 0:256], func=AF.Copy),
               reads=[pv], writes=[VV])
        dma("sp", QTd[:, :, tl.g0:tl.g0 + T], QST.ap[:, :, :T], [QST], [("QT", tl.tid)], "qts")
        hook_p2()

    def main_attnB(job, hook_mid, hook_p2):
        ph, tl = job.sub, job.tile
        T = tl.T
        nkc = (NTOK // 128) if tl.kind == "lat" else (CTX // 128)
        seq = [(h, c) for h in range(NH) for c in range(nkc)]
        n = len(seq)
        scale = float(HD) ** -0.5

        def S(i):
            h, c = seq[i]
            psb = PS[i % 2]
            op("pe", lambda e: e.matmul(psb.ap[:, :T], lhsT=KT.ap[:, h // 4, c * 128:(c + 1) * 128], rhs=HBf.ap[:, h, :T],
                                        start=True, stop=True),
               reads=[KT, HBf], writes=[psb])

        def EPV(i):
            h, c = seq[i]
            psb = PS[i % 2]
            pb = PBs[i % 4]
            op("act", lambda e: e.activation(out=pb.ap[:, :T], in_=psb.ap[:, :T], func=AF.Exp, scale=scale),
               reads=[psb], writes=[pb])
            ob = PS[2 + h % 2]
            db = PS[7]
            kv = h // 4
            op("pe", lambda e: e.matmul(ob.ap[:, :T], lhsT=VV.ap[:, c, kv * 128:(kv + 1) * 128], rhs=pb.ap[:, :T],
                                        start=(c == 0), stop=(c == nkc - 1)),
               reads=[VV, pb], writes=[ob])
            op("pe", lambda e: e.matmul(db.ap[:, :T], lhsT=ONES.ap, rhs=pb.ap[:, :T], start=(c == 0), stop=(c == nkc - 1)),
               reads=[ONES, pb], writes=[db])
            if c == nkc - 1:
                op("dve", lambda e: e.reciprocal(out=RDEN.ap[:, :T], in_=db.ap[:, :T]), reads=[db], writes=[RDEN])
                op("dve", lambda e: e.tensor_tensor(out=OBc[h].ap[:, :T], in0=ob.ap[:, :T], in1=RDEN.ap[:, :T], op=ALU.mult),
                   reads=[ob, RDEN], writes=[OBc[h]])

        S(0)
        for i in range(n):
            if i + 1 < n:
                S(i + 1)
            EPV(i)
            if i == n // 2:
                hook_mid()
        hook_p2()
        out_proj(job, lambda k, dc: (ph.Wo[k // 2], ph.Wo[k // 2].ap[:, k % 2, dc * 128:(dc + 1) * 128]), OBc, NH)

    MAIN = {"ffn": main_ffn, "final": main_final, "convA": main_convA, "convB": main_convB,
            "attnA": main_attnA, "attnB": main_attnB}

    wl = {}
    for ph in phases:
        wl[id(ph)] = setup_weights(ph) if ph.ty != "final" else ([], [])
    r1, r2 = wl[id(phases[0])]
    for f in r1 + r2:
        f()
    emit_load(jobs[0])
    emit_n1(jobs[0])
    emit_n2(jobs[0])
    for k, job in enumerate(jobs):
        nxt = jobs[k + 1] if k + 1 < len(jobs) else None
        nph = nxt.sub if (nxt is not None and job.last) else None
        late = nxt is not None and (job.sub.ty == "final" or nxt.sub.ty == "attnB")
        if nxt is not None and not late:
            emit_load(nxt)

        def hook_mid(nxt=nxt, late=late):
            if nxt is not None and not late:
                emit_n1(nxt)

        def hook_p2(nxt=nxt, nph=nph, late=late):
            if nph is not None:
                for f in wl[id(nph)][0]:
                    f()
            if nxt is not None:
                if late:
                    emit_load(nxt)
                    emit_n1(nxt)
                emit_n2(nxt)

        MAIN[job.sub.ty](job, hook_mid, hook_p2)
        if nph is not None:
            for f in wl[id(nph)][1]:
                f()
    tr.finish()
    es.close()
    build_program.stats = (tr.ninst, tr.nsem)
    return nc


def _rope_tables():
    t = np.arange(SEQ)
    row = (t // GRID_W).astype(np.float32)
    col = (t % GRID_W).astype(np.float32)
    nf = HD // 4
    inv = (np.float32(10000.0) ** (-(np.arange(nf, dtype=np.float32) / np.float32(nf)))).astype(np.float32)
    ar = (row[None, :] * inv[:, None]).astype(np.float32)
    ac = (col[None, :] * inv[:, None]).astype(np.float32)
    cosT = np.concatenate([np.cos(ar), np.cos(ar), np.cos(ac), np.cos(ac)], 0).astype(np.float32)
    sinT = np.concatenate([-np.sin(ar), np.sin(ar), -np.sin(ac), np.sin(ac)], 0).astype(np.float32)
    perm = np.zeros((128, 128), np.float32)
    for m in range(128):
        perm[m ^ 32, m] = 1.0
    return np.ascontiguousarray(cosT), np.ascontiguousarray(sinT), perm


def _fm(v):
    v = np.asarray(v, np.float32)
    lead = v.shape[:-1]
    return np.moveaxis(v.reshape(lead + (KC, 128)), -1, 0)


def make_in_maps(x, c, ctx, c_ctx, ada_w, ada_b, norm_g, final_g, ffn_w_in, ffn_w_out, conv_w_in, conv_w,
                 conv_w_out, attn_w_qkv, attn_q_g, attn_k_g, attn_w_o):
    f = np.float32
    cosT, sinT, perm = _rope_tables()
    ada_w_h = np.ascontiguousarray(
        np.asarray(ada_w, f).reshape(DEPTH, KC, 128, 18, 512).transpose(0, 3, 2, 1, 4)).reshape(DEPTH, 18, 128, KC * 512)
    wi = np.asarray(ffn_w_in, f).reshape(DEPTH, 2, KC, 128, 2, FC, 128)
    fwin_h = np.ascontiguousarray(wi.transpose(0, 1, 5, 3, 2, 4, 6)).reshape(DEPTH, 2, FC, 128, KC * 256)
    fwout_h = np.ascontiguousarray(np.asarray(ffn_w_out, f).reshape(DEPTH, 2, FC, 128, D))
    cwin_h = np.ascontiguousarray(np.asarray(conv_w_in, f).reshape(2, KC, 128, 3 * D).transpose(0, 2, 1, 3))
    cwout_h = np.ascontiguousarray(np.asarray(conv_w_out, f).reshape(2, KC, 128, D).transpose(0, 2, 1, 3))
    aqkv_h = np.ascontiguousarray(np.asarray(attn_w_qkv, f).reshape(2, KC, 128, 1536).transpose(0, 2, 1, 3))
    awo_h = np.ascontiguousarray(np.asarray(attn_w_o, f).reshape(2, KC, 128, D).transpose(0, 2, 1, 3))
    adab = np.asarray(ada_b, f).reshape(DEPTH, 72, 128).transpose(2, 0, 1).reshape(128, DEPTH * 72)
    ng = _fm(norm_g).reshape(128, 96)
    fg = _fm(final_g).reshape(128, 8)
    cw = _fm(conv_w).reshape(128, 48)
    qkg = np.stack([np.asarray(attn_q_g, f), np.asarray(attn_k_g, f)], 1).reshape(4, 128).T
    cctx = _fm(c_ctx).reshape(128, 8)
    xf = np.asarray(x, f)
    cf = np.asarray(ctx, f)
    maps = []
    for b in range(NCORES):
        cc = np.stack([_fm(np.asarray(c, f)[b]).reshape(128, 8), cctx], -1).reshape(128, 16)
        small = np.ascontiguousarray(np.concatenate([cc, adab, ng, fg, cw, qkg], 1).astype(f))
        assert small.shape == (128, NSMALL)
        maps.append({
            "xT": np.ascontiguousarray(xf[b].T.reshape(KC, 128, SEQ)),
            "ctxT": np.ascontiguousarray(cf[b].T.reshape(KC, 128, CTX)),
            "smallp": small, "perm": perm, "cosT": cosT, "sinT": sinT,
            "ada_w": ada_w_h, "ffn_w_in": fwin_h, "ffn_w_out": fwout_h, "conv_w_in": cwin_h,
            "conv_w_out": cwout_h, "attn_w_qkv": aqkv_h, "attn_w_o": awo_h,
        })
    return maps


_NC_CACHE = {}


def run(inputs, n_sub=12, cores=NCORES, trace=False):
    if n_sub not in _NC_CACHE:
        _NC_CACHE[n_sub] = build_program(n_sub)
    nc = _NC_CACHE[n_sub]
    maps = make_in_maps(**inputs)[:cores]
    res = run_bass_kernel_spmd(nc, maps, core_ids=list(range(cores)), trace=trace)
    outs = [np.asarray(r["outT"]).reshape(D, SEQ).T for r in res.results]
    return np.ascontiguousarray(np.stack(outs, 0).astype(np.float32)), res


def kernel(**inputs):
    out, _ = run(inputs)
    return out
```

```python
import numpy as np
from contextlib import ExitStack
import concourse.bass as bass
import concourse.mybir as mybir
from concourse.bass_utils import run_bass_kernel_spmd

F32 = mybir.dt.float32
BF16 = mybir.dt.bfloat16
AF = mybir.ActivationFunctionType
ALU = mybir.AluOpType

D = 1024
KC = 8
DFF = 2816
FC = 22
SEQ = 4096
CTX = 256
NTOK = SEQ + CTX
DEPTH = 4
HD = 128
NH = 8
NKV = 2
GRID_W = 64
EPS = 1e-6
NCORES = 8

C_ONES = 0
C_ONESM = 256
C_ONESH = 512
C_PERM = 768
C_SMALL = 1280
ARENA = 10240
R1 = ARENA
R1_SZ = 90112
R2 = ARENA + R1_SZ
R2_SZ = 45056
XB = R2 + R2_SZ
HB = XB + 16384
ACTB = HB + 8192
SQ = ACTB + 22528
RSTD = SQ + 2048
XR = RSTD + 2048
SL = XR + 4096
E2B = SL + 4096
SB_END = E2B + 3 * 2048
assert SB_END <= 212480
GR = 512

SAME_ENGINE_SYNC = True
SEM_ROLL = 30000


class Buf:
    __slots__ = ("ap", "keys")

    def __init__(self, ap, keys):
        self.ap = ap
        self.keys = keys


def _keys(lst):
    out = []
    for x in lst:
        if isinstance(x, Buf):
            out.extend(x.keys)
        else:
            out.append(x)
    return out


class Trk:
    def __init__(self, nc, es):
        self.nc = nc
        self.es = es
        self.eng = {"pe": nc.tensor, "act": nc.scalar, "dve": nc.vector, "pool": nc.gpsimd, "sp": nc.sync}
        self.esem = {}
        self.ecnt = {}
        self.pe_sems = set()
        self.waited = {e: {} for e in self.eng}
        self.state = {}
        self.dsem = {}
        self.nsem = 0
        self.ninst = 0
        for e in ("pe", "act", "dve", "pool"):
            self._roll(e)

    def _newsem(self, name):
        self.nsem += 1
        return self.es.enter_context(self.nc.semaphore(f"{name}{self.nsem}"))

    def _roll(self, e):
        self.esem[e] = self._newsem(e)
        self.ecnt[e] = 0
        if e == "pe":
            self.pe_sems.add(id(self.esem[e]))

    def _deps(self, rk, wk):
        deps = {}
        st_ = self.state
        for k in rk:
            st = st_.get(k)
            if st is not None and st[0] is not None:
                ts = st[0]
                i = id(ts[0])
                d = deps.get(i)
                if d is None or d[1] < ts[1]:
                    deps[i] = ts
        for k in wk:
            st = st_.get(k)
            if st is not None:
                ts = st[0]
                if ts is not None:
                    i = id(ts[0])
                    d = deps.get(i)
                    if d is None or d[1] < ts[1]:
                        deps[i] = ts
                for i, ts in st[1].items():
                    d = deps.get(i)
                    if d is None or d[1] < ts[1]:
                        deps[i] = ts
        return deps

    def _wait(self, e, deps):
        w = self.waited[e]
        eng = self.eng[e]
        for i, (sem, val) in deps.items():
            if e == "pe" and i in self.pe_sems:
                continue
            if (not SAME_ENGINE_SYNC) and e in self.esem and sem is self.esem[e]:
                continue
            if w.get(i, 0) < val:
                eng.wait_ge(sem, val)
                w[i] = val

    def _commit(self, ts, rk, wk):
        st_ = self.state
        i = id(ts[0])
        for k in wk:
            st_[k] = [ts, {}]
        for k in rk:
            st = st_.get(k)
            if st is None:
                st = st_[k] = [None, {}]
            st[1][i] = ts

    def op(self, e, fn, reads=(), writes=()):
        rk = _keys(reads)
        wk = _keys(writes)
        self._wait(e, self._deps(rk, wk))
        ins = fn(self.eng[e])
        if self.ecnt[e] >= SEM_ROLL:
            self._roll(e)
        self.ecnt[e] += 1
        ins.then_inc(self.esem[e], 1)
        self._commit((self.esem[e], self.ecnt[e]), rk, wk)
        self.ninst += 1

    def dma(self, e, out, in_, reads, writes, semkey, **kw):
        rk = _keys(reads)
        wk = _keys(writes)
        deps = self._deps(rk, wk)
        ds = self.dsem.get(semkey)
        if ds is None:
            ds = self.dsem[semkey] = [self._newsem("d"), 0]
        if ds[1] > 0:
            deps[id(ds[0])] = (ds[0], ds[1])
        self._wait(e, deps)
        self.eng[e].dma_start(out=out, in_=in_, **kw).then_inc(ds[0], 16)
        ds[1] += 16
        self._commit((ds[0], ds[1]), rk, wk)
        self.ninst += 1

    def finish(self):
        sp = self.eng["sp"]
        for ds in self.dsem.values():
            if ds[1] > 0:
                sp.wait_ge(ds[0], ds[1])


class Tile:
    def __init__(self, b, idx, kind, g0, T, bpc):
        self.b = b
        self.idx = idx
        self.tid = b * 9 + idx
        self.kind = kind
        self.g0 = g0
        self.T = T
        self.k = b if kind == "lat" else bpc


class Job:
    def __init__(self, sub, tile, first, last):
        self.sub = sub
        self.tile = tile
        self.first = first
        self.last = last
        self.par = 0


def build_program(n_sub=12, bpc=1):
    nc = bass.Bass("TRN2", target_bir_lowering=False)
    es = ExitStack()
    NK = bpc + 1
    NKP = 2 if NK == 2 else 4
    NSMALL = 8 * NKP + 288 + 96 + 8 + 48 + 4
    C_SCC = C_SMALL + 4 * NSMALL
    C_MODS = C_SCC + 4 * 8 * NKP
    C_A = C_MODS + 4 * DEPTH * NK * 72
    C_GH = C_A + 4 * 12 * NK * 8
    C_ZERO = C_GH + 4 * 8 * NK * 8
    C_EPS = C_ZERO + 32
    assert C_EPS + 4 <= ARENA
    CTXT = [Tile(b, 0, "ctx", 0, CTX, bpc) for b in range(bpc)]
    LATT = [[Tile(b, 1 + i, "lat", CTX + 512 * i, 512, bpc) for i in range(8)] for b in range(bpc)]

    def din(name, shape, dt=F32):
        return nc.dram_tensor(name, list(shape), dt, kind="ExternalInput").ap()

    xT = din("xT", [bpc, KC, 128, SEQ])
    ctxT = din("ctxT", [bpc, KC, 128, CTX])
    smallp = din("smallp", [128, NSMALL])
    permd = din("perm", [128, 128])
    cosd = din("cosT", [128, SEQ])
    sind = din("sinT", [128, SEQ])
    adaw = din("ada_w", [DEPTH, 18, 128, KC * 512])
    fwin = din("ffn_w_in", [DEPTH, 2, FC, 128, KC * 256])
    fwout = din("ffn_w_out", [DEPTH, 2, FC, 128, D])
    cwin = din("conv_w_in", [2, 128, KC, 3 * D])
    cwout = din("conv_w_out", [2, 128, KC, D])
    aqkv = din("attn_w_qkv", [2, 128, KC, 1536])
    awo = din("attn_w_o", [2, 128, KC, D])
    outT = nc.dram_tensor("outT", [bpc, KC, 128, SEQ], F32, kind="ExternalOutput").ap()
    Xd = nc.dram_tensor("Xs", [bpc, KC, 128, NTOK], F32, kind="Internal").ap()
    UW = NTOK + 4
    Ud = nc.dram_tensor("Us", [bpc, KC, 128, UW], F32, kind="Internal").ap()
    QTd = nc.dram_tensor("QTs", [bpc, 128, NH, NTOK], BF16, kind="Internal").ap()

    AR = es.enter_context(nc.sbuf_tensor("AR", [128, SB_END // 4], F32))
    PSt = es.enter_context(nc.psum_tensor("PS", [128, 8, 512], F32))
    tr = Trk(nc, es)
    op = tr.op
    dma = tr.dma

    def sview(off, dims, dt):
        n = int(np.prod(dims))
        esz = 4 if dt is F32 else 2
        nb = n * esz
        assert off % 4 == 0 and nb % 4 == 0 and off + nb <= SB_END, (off, dims)
        ap = AR[:, off // 4:(off + nb) // 4]
        if dt is BF16:
            ap = ap.bitcast(BF16)
        if len(dims) == 2:
            ap = ap.rearrange("p (a b) -> p a b", b=dims[1])
        elif len(dims) == 3:
            ap = ap.rearrange("p (a b c) -> p a b c", b=dims[1], c=dims[2])
        keys = [("S", g) for g in range(off // GR, (off + nb - 1) // GR + 1)]
        return Buf(ap, keys)

    PS = [Buf(PSt[:, b, :], [("P", b)]) for b in range(8)]

    ONES = sview(C_ONES, [128], BF16)
    ONESM = sview(C_ONESM, [128], BF16)
    ONESH = sview(C_ONESH, [128], BF16)
    PERM = sview(C_PERM, [128], F32)
    SMALL = sview(C_SMALL, [NSMALL], F32)
    SCC = sview(C_SCC, [KC, NKP], F32)
    MODS = sview(C_MODS, [DEPTH * NK, 72], F32)
    AV = sview(C_A, [12 * NK, 8], F32)
    GH = sview(C_GH, [8 * NK, 8], F32)
    ZERO = sview(C_ZERO, [8, 1], F32)
    EPSC = sview(C_EPS, [1], F32)

    XBf = sview(XB, [KC, 512], F32)
    XBc = [sview(XB + kc * 2048, [512], F32) for kc in range(KC)]
    HBf = sview(HB, [KC, 512], BF16)
    HBc = [sview(HB + kc * 1024, [512], BF16) for kc in range(KC)]
    ACTc = [sview(ACTB + fc * 1024, [512], BF16) for fc in range(FC)]
    SQs = [sview(SQ + i * 1024, [512], BF16) for i in range(2)]
    RSTDb = sview(RSTD, [512], F32)
    XRs = [sview(XR + i * 2048, [512], F32) for i in range(2)]
    SLs = [sview(SL + i * 2048, [512], F32) for i in range(2)]
    E2s = [sview(E2B + i * 2048, [512], F32) for i in range(3)]
    e2ctr = [0]

    O_CC = 0
    O_ADAB = 8 * NKP
    O_NG = O_ADAB + 288
    O_FG = O_NG + 96
    O_CW = O_FG + 8
    O_QKG = O_CW + 48

    def smallcol(o, n=1):
        return SMALL.ap[:, o:o + n]

    op("pool", lambda e: e.memset(ONES.ap, 1.0), writes=[ONES])
    op("pool", lambda e: e.memset(ONESM.ap, 1.0 / 1024.0), writes=[ONESM])
    op("pool", lambda e: e.memset(ONESH.ap, 1.0 / 128.0), writes=[ONESH])
    op("pool", lambda e: e.memset(ZERO.ap, 0.0), writes=[ZERO])
    op("pool", lambda e: e.memset(EPSC.ap, EPS), writes=[EPSC])
    dma("sp", SMALL.ap, smallp, [], [SMALL], "small")
    dma("sp", PERM.ap, permd, [], [PERM], "perm")
    for b_ in range(bpc):
        for col in (0, CTX + 1, CTX + 2, UW - 1):
            dma("sp", Ud[b_, :, :, col:col + 1].rearrange("k p o -> p k o"), ZERO.ap, [ZERO], [("Upad", col)], "upad",
                allow_slow_non_contiguous=True)
    op("act", lambda e: e.activation(out=SCC.ap.rearrange("p k t -> p (k t)"), in_=smallcol(O_CC, 8 * NKP), func=AF.Silu),
       reads=[SMALL], writes=[SCC])

    STG = [sview(R1 + i * 16384, [KC, 512], F32) for i in range(2)]
    nst = 0
    for i in range(DEPTH):
        for jb in range(18):
            stg = STG[nst % 2]
            dma("sp", stg.ap.rearrange("p k n -> p (k n)"), adaw[i, jb], [], [stg], ("ada", nst % 2))
            nst += 1
            for j4 in range(4):
                j = jb * 4 + j4
                for kc in range(KC):
                    op("pe", lambda e, j=j, j4=j4, kc=kc, stg=stg: e.matmul(
                        PSt[:, 7, NKP * j:NKP * j + NKP], lhsT=stg.ap[:, kc, j4 * 128:(j4 + 1) * 128],
                        rhs=SCC.ap[:, kc, :], start=(kc == 0), stop=(kc == KC - 1)),
                       reads=[stg, SCC], writes=[PS[7]])
        psv = PSt[:, 7, 0:72 * NKP].rearrange("p (j t) -> p j t", t=NKP)
        for kind in range(NK):
            op("dve", lambda e, i=i, kind=kind, psv=psv: e.tensor_tensor(
                out=MODS.ap[:, i * NK + kind, :], in0=psv[:, :, kind], in1=smallcol(O_ADAB + i * 72, 72), op=ALU.add),
               reads=[PS[7], SMALL], writes=[MODS])
        for s in range(3):
            for kind in range(NK):
                m = MODS.ap[:, i * NK + kind, :]
                op("dve", lambda e, m=m, i=i, s=s, kind=kind: e.scalar_tensor_tensor(
                    out=AV.ap[:, (i * 3 + s) * NK + kind, :], in0=m[:, (s * 3 + 1) * 8:(s * 3 + 1) * 8 + 8], scalar=1.0,
                    in1=smallcol(O_NG + (i * 3 + s) * 8, 8), op0=ALU.add, op1=ALU.mult),
                   reads=[MODS, SMALL], writes=[AV])
                if s != 1:
                    jj = 0 if s == 0 else 1
                    op("dve", lambda e, m=m, i=i, s=s, jj=jj, kind=kind: e.tensor_scalar(
                        out=GH.ap[:, (i * 2 + jj) * NK + kind, :], in0=m[:, (s * 3 + 2) * 8:(s * 3 + 2) * 8 + 8],
                        scalar1=0.5, scalar2=None, op0=ALU.mult),
                       reads=[MODS], writes=[GH])

    def mod_A(i, s, kind):
        return AV.ap[:, (i * 3 + s) * NK + kind, :]

    def mod_S(i, s, kind):
        return MODS.ap[:, i * NK + kind, (s * 3) * 8:(s * 3) * 8 + 8]

    def mod_G(i, s, kind):
        if s == 1:
            return MODS.ap[:, i * NK + kind, (s * 3 + 2) * 8:(s * 3 + 2) * 8 + 8]
        return GH.ap[:, (i * 2 + (0 if s == 0 else 1)) * NK + kind, :]

    class Sub:
        pass

    subs = []
    for i in range(DEPTH):
        is_attn = (i % 2) == 1
        last = i == DEPTH - 1
        for s in range(3):
            sb = Sub()
            sb.layer, sb.s = i, s
            if s == 0:
                sb.types = ["ffn"]
                sb.ctx = True
            elif s == 2:
                sb.types = ["ffn"]
                sb.ctx = not last
            elif is_attn:
                sb.types = ["attnA", "attnB"]
                sb.ctx = True
                sb.ctx_out = not last
            else:
                sb.types = ["convA", "convB"]
                sb.ctx = not last
            subs.append(sb)
    subs = subs[:n_sub]

    class Phase:
        pass

    phases = []
    for si, sb in enumerate(subs):
        if sb.types[0] == "attnA":
            plist = [(ty, [b]) for b in range(bpc) for ty in sb.types]
        else:
            plist = [(ty, list(range(bpc))) for ty in sb.types]
        seen = {}
        for ty, bl in plist:
            ph = Phase()
            ph.ty = ty
            ph.sub = sb
            ph.layer, ph.s = sb.layer, sb.s
            ph.first_sub = (si == 0)
            ph.wsrc = seen.get(ty)
            seen.setdefault(ty, ph)
            use_ctx = sb.ctx
            if ty == "attnB":
                use_ctx = sb.ctx_out
            ph.tiles = []
            for b in bl:
                ph.tiles += ([CTXT[b]] if use_ctx else []) + LATT[b]
            phases.append(ph)
    phf = Phase()
    phf.ty = "final"
    phf.layer, phf.s = 0, 0
    phf.first_sub = (len(subs) == 0)
    phf.wsrc = None
    phf.tiles = [t for b in range(bpc) for t in LATT[b]]
    phases.append(phf)

    jobs = []
    for ph in phases:
        n = len(ph.tiles)
        for ti, tl in enumerate(ph.tiles):
            jb = Job(ph, tl, ti == 0, ti == n - 1)
            jb.par = len(jobs) % 2
            jobs.append(jb)

    def src_full(ph, tl):
        if ph.first_sub and ph.ty in ("ffn", "final"):
            if tl.kind == "lat":
                return xT[tl.b, :, :, tl.g0 - CTX:tl.g0 - CTX + tl.T].rearrange("k p t -> p k t"), []
            return ctxT[tl.b, :, :, 0:tl.T].rearrange("k p t -> p k t"), []
        return Xd[tl.b, :, :, tl.g0:tl.g0 + tl.T].rearrange("k p t -> p k t"), [("X", tl.tid, dc) for dc in range(KC)]

    def src_chunk(ph, tl, dc):
        if ph.first_sub and ph.ty == "ffn":
            if tl.kind == "lat":
                return xT[tl.b, dc, :, tl.g0 - CTX:tl.g0 - CTX + tl.T], []
            return ctxT[tl.b, dc, :, 0:tl.T], []
        return Xd[tl.b, dc, :, tl.g0:tl.g0 + tl.T], [("X", tl.tid, dc)]

    def dst_chunk(tl, dc):
        return Xd[tl.b, dc, :, tl.g0:tl.g0 + tl.T], [("X", tl.tid, dc)]

    def ucol(tl):
        return (1 + tl.g0) if tl.kind == "ctx" else (3 + tl.g0)

    def wdma(out_ap, in_ap, buf, idx):
        dma("pool", out_ap, in_ap, [], [buf], ("w", idx))

    def setup_weights(ph):
        r1, r2 = [], []
        ty = ph.ty
        if ph.wsrc is not None:
            for a in ("Win", "Wout", "Wp", "Wo"):
                if hasattr(ph.wsrc, a):
                    setattr(ph, a, getattr(ph.wsrc, a))
            return r1, r2
        if ty == "ffn":
            i, j = ph.layer, (0 if ph.s == 0 else 1)
            ph.Win = [sview(R1 + fc * 4096, [KC, 256], BF16) for fc in range(FC)]
            ph.Wout = [sview(R2 + fp * 4096, [2, D], BF16) for fp in range(FC // 2)]
            for fc in range(FC):
                r1.append(lambda fc=fc: wdma(ph.Win[fc].ap.rearrange("p k n -> p (k n)"), fwin[i, j, fc], ph.Win[fc], fc))
            for fp in range(FC // 2):
                r2.append(lambda fp=fp: wdma(ph.Wout[fp].ap, fwout[i, j, 2 * fp:2 * fp + 2].rearrange("f p n -> p f n"),
                                            ph.Wout[fp], FC + fp))
        elif ty == "convA":
            jj = ph.layer // 2
            ph.Wp = [sview(R1 + pi * 8192, [KC, 512], BF16) for pi in range(4)]
            for pi in range(4):
                r1.append(lambda pi=pi: wdma(ph.Wp[pi].ap, cwin[jj, :, :, D + pi * 512:D + (pi + 1) * 512], ph.Wp[pi], pi))
        elif ty == "convB":
            jj = ph.layer // 2
            ph.Wp = [sview(R1 + pi * 8192, [KC, 512], BF16) for pi in range(2)]
            ph.Wo = [sview(R2 + pi * 4096, [2, D], BF16) for pi in range(4)]
            for pi in range(2):
                r1.append(lambda pi=pi: wdma(ph.Wp[pi].ap, cwin[jj, :, :, pi * 512:(pi + 1) * 512], ph.Wp[pi], pi))
            for pi in range(4):
                r2.append(lambda pi=pi: wdma(ph.Wo[pi].ap, cwout[jj, :, 2 * pi:2 * pi + 2, :], ph.Wo[pi], FC + pi))
        elif ty == "attnA":
            jj = ph.layer // 2
            ph.Wp = [sview(R1 + pi * 8192, [KC, 512], BF16) for pi in range(3)]
            for pi in range(3):
                r1.append(lambda pi=pi: wdma(ph.Wp[pi].ap, aqkv[jj, :, :, pi * 512:(pi + 1) * 512], ph.Wp[pi], pi))
        elif ty == "attnB":
            jj = ph.layer // 2
            ph.Wo = [sview(R2 + pi * 4096, [2, D], BF16) for pi in range(4)]
            for pi in range(4):
                r2.append(lambda pi=pi: wdma(ph.Wo[pi].ap, awo[jj, :, 2 * pi:2 * pi + 2, :], ph.Wo[pi], FC + pi))
        return r1, r2

    KT = sview(R1 + 24576, [NKV, NTOK], BF16)
    VV = sview(R1 + 24576 + 17408, [NTOK // 128, 256], BF16)
    PBs = [sview(R1 + 59392 + i * 1024, [512], BF16) for i in range(4)]
    OBUF = sview(R2 + 16384, [NH, 512], BF16)
    OBc = [sview(R2 + 16384 + h * 1024, [512], BF16) for h in range(NH)]
    RDEN = sview(R2 + 24576, [512], F32)
    QST = sview(ACTB, [NH, 512], BF16)
    QRAW = [sview(ACTB + 8192 + i * 2048, [512], F32) for i in range(2)]
    QN = [sview(ACTB + 12288 + i * 2048, [512], F32) for i in range(2)]
    COSb = sview(ACTB + 16384, [512], F32)
    SINb = sview(ACTB + 18432, [512], F32)
    RQ = sview(ACTB + 20480, [512], F32)
    UBs = [sview(R1 + 32768 + i * 16448, [KC, 514], F32) for i in range(2)]
    CT = [sview(R1 + 65664 + i * 2048, [512], F32) for i in range(2)]
    CSL = [sview(R1 + 69760 + i * 2048, [512], F32) for i in range(2)]
    ZB = [sview(R2 + 16384 + kc * 1024, [512], BF16) for kc in range(KC)]

    def emit_load(job):
        ph, tl = job.sub, job.tile
        T = tl.T
        if ph.ty == "attnB":
            dma("sp", HBf.ap[:, :, :T], QTd[tl.b, :, :, tl.g0:tl.g0 + T], [("QT", tl.tid)], [HBf], "qtl")
            return
        ap, keys = src_full(ph, tl)
        dma("sp", XBf.ap[:, :, :T], ap, keys, [XBf], "xl")
        if ph.ty == "convB":
            ub = UBs[job.par]
            c0 = ucol(tl) - 1
            rk = [("U", t2.tid, dc) for t2 in ph.tiles if t2.kind == tl.kind and t2.b == tl.b and abs(t2.tid - tl.tid) <= 1
                  for dc in range(KC)]
            rk += [("Upad", c) for c in (0, CTX + 1, CTX + 2, UW - 1)]
            dma("sp", ub.ap[:, :, :T + 2], Ud[tl.b, :, :, c0:c0 + T + 2].rearrange("k p t -> p k t"), rk, [ub], ("ul", job.par))

    def emit_n1(job):
        ph, tl = job.sub, job.tile
        if ph.ty == "attnB":
            return
        T = tl.T
        for kc in range(KC):
            s = SQs[kc % 2]
            op("act", lambda e, s=s, kc=kc: e.activation(out=s.ap[:, :T], in_=XBc[kc].ap[:, :T], func=AF.Square),
               reads=[XBc[kc]], writes=[s])
            op("pe", lambda e, s=s, kc=kc: e.matmul(PS[6].ap[:, :T], lhsT=ONESM.ap, rhs=s.ap[:, :T],
                                                  start=(kc == 0), stop=(kc == KC - 1)),
               reads=[s, ONESM], writes=[PS[6]])
        op("act", lambda e: e.activation(out=RSTDb.ap[:, :T], in_=PS[6].ap[:, :T], func=AF.Sqrt, bias=EPSC.ap),
           reads=[PS[6], EPSC], writes=[RSTDb])
        op("dve", lambda e: e.reciprocal(out=RSTDb.ap[:, :T], in_=RSTDb.ap[:, :T]), reads=[RSTDb], writes=[RSTDb])

    def emit_n2(job):
        ph, tl = job.sub, job.tile
        if ph.ty in ("attnB", "final"):
            return
        T = tl.T
        A = mod_A(ph.layer, ph.s, tl.k)
        S = mod_S(ph.layer, ph.s, tl.k)
        for kc in range(KC):
            xr = XRs[kc % 2]
            op("dve", lambda e, xr=xr, kc=kc: e.tensor_tensor(out=xr.ap[:, :T], in0=XBc[kc].ap[:, :T], in1=RSTDb.ap[:, :T],
                                                            op=ALU.mult),
               reads=[XBc[kc], RSTDb], writes=[xr])
            op("act", lambda e, xr=xr, kc=kc: e.activation(out=HBc[kc].ap[:, :T], in_=xr.ap[:, :T], func=AF.Identity,
                                                         scale=A[:, kc:kc + 1], bias=S[:, kc:kc + 1]),
               reads=[xr, AV, MODS], writes=[HBc[kc]])

    def e2_reload(job, dc):
        ph, tl = job.sub, job.tile
        slot = E2s[(e2ctr[0] + dc) % 3]
        ap, keys = src_chunk(ph, tl, dc)
        dma("sp", slot.ap[:, :tl.T], ap, keys, [slot], ("e2l", (e2ctr[0] + dc) % 3))

    def e2_update(job, dc, yb):
        ph, tl = job.sub, job.tile
        T = tl.T
        si = (e2ctr[0] + dc) % 3
        slot = E2s[si]
        G = mod_G(ph.layer, ph.s, tl.k)
        op("dve", lambda e: e.scalar_tensor_tensor(out=slot.ap[:, :T], in0=yb.ap[:, :T], scalar=G[:, dc:dc + 1],
                                                   in1=slot.ap[:, :T], op0=ALU.mult, op1=ALU.add),
           reads=[yb, slot, MODS, GH], writes=[slot])
        ap, keys = dst_chunk(tl, dc)
        dma("sp", ap, slot.ap[:, :T], [slot], keys, ("e2s", si))

    def out_proj(job, wfn, rhs_list, nk):
        tl = job.tile
        T = tl.T
        e2_reload(job, 0)
        e2_reload(job, 1)
        for dc in range(KC):
            yb = PS[4 + dc % 2]
            for k in range(nk):
                wb, wap = wfn(k, dc)
                op("pe", lambda e, wap=wap, k=k, yb=yb: e.matmul(yb.ap[:, :T], lhsT=wap, rhs=rhs_list[k].ap[:, :T],
                                                               start=(k == 0), stop=(k == nk - 1)),
                   reads=[wb, rhs_list[k]], writes=[yb])
            e2_update(job, dc, yb)
            if dc + 2 < KC:
                e2_reload(job, dc + 2)
        e2ctr[0] += KC

    def main_ffn(job, hook_mid, hook_p2):
        ph, tl = job.sub, job.tile
        T = tl.T
        for fc in range(FC):
            b = fc % 2
            w = ph.Win[fc]
            for half in range(2):
                psb = PS[2 * half + b]
                for kc in range(KC):
                    op("pe", lambda e, w=w, half=half, kc=kc, psb=psb: e.matmul(
                        psb.ap[:, :T], lhsT=w.ap[:, kc, half * 128:(half + 1) * 128], rhs=HBc[kc].ap[:, :T],
                        start=(kc == 0), stop=(kc == KC - 1)),
                       reads=[w, HBc[kc]], writes=[psb])
            sl = SLs[b]
            op("act", lambda e, sl=sl, b=b: e.activation(out=sl.ap[:, :T], in_=PS[b].ap[:, :T], func=AF.Silu),
               reads=[PS[b]], writes=[sl])
            op("dve", lambda e, sl=sl, b=b, fc=fc: e.tensor_tensor(out=ACTc[fc].ap[:, :T], in0=sl.ap[:, :T],
                                                                 in1=PS[2 + b].ap[:, :T], op=ALU.mult),
               reads=[sl, PS[2 + b]], writes=[ACTc[fc]])
            if fc == 6:
                hook_mid()
        hook_p2()
        out_proj(job, lambda k, dc: (ph.Wout[k // 2], ph.Wout[k // 2].ap[:, k % 2, dc * 128:(dc + 1) * 128]), ACTc, FC)

    def main_final(job, hook_mid, hook_p2):
        tl = job.tile
        T = tl.T
        t0 = tl.g0 - CTX
        for dc in range(KC):
            si = (e2ctr[0] + dc) % 3
            slot = E2s[si]
            op("dve", lambda e, dc=dc, slot=slot: e.scalar_tensor_tensor(
                out=slot.ap[:, :T], in0=XBc[dc].ap[:, :T], scalar=smallcol(O_FG + dc), in1=RSTDb.ap[:, :T],
                op0=ALU.mult, op1=ALU.mult),
               reads=[XBc[dc], SMALL, RSTDb], writes=[slot])
            dma("sp", outT[tl.b, dc, :, t0:t0 + T], slot.ap[:, :T], [slot], [("OUT", tl.tid, dc)], ("e2s", si))
        e2ctr[0] += KC
        hook_mid()
        hook_p2()

    def main_convA(job, hook_mid, hook_p2):
        ph, tl = job.sub, job.tile
        T = tl.T
        c0 = ucol(tl)
        for c in range(KC):
            b = c % 2
            for half in range(2):
                w = ph.Wp[2 * half + c // 4]
                psb = PS[2 * half + b]
                for kc in range(KC):
                    op("pe", lambda e, w=w, kc=kc, psb=psb, c=c: e.matmul(
                        psb.ap[:, :T], lhsT=w.ap[:, kc, (c % 4) * 128:(c % 4 + 1) * 128], rhs=HBc[kc].ap[:, :T],
                        start=(kc == 0), stop=(kc == KC - 1)),
                       reads=[w, HBc[kc]], writes=[psb])
            cs = CSL[b]
            op("act", lambda e, cs=cs, b=b: e.activation(out=cs.ap[:, :T], in_=PS[b].ap[:, :T], func=AF.Copy),
               reads=[PS[b]], writes=[cs])
            si = (e2ctr[0] + c) % 3
            slot = E2s[si]
            op("dve", lambda e, cs=cs, b=b, slot=slot: e.tensor_tensor(out=slot.ap[:, :T], in0=cs.ap[:, :T],
                                                                     in1=PS[2 + b].ap[:, :T], op=ALU.mult),
               reads=[cs, PS[2 + b]], writes=[slot])
            dma("sp", Ud[tl.b, c, :, c0:c0 + T], slot.ap[:, :T], [slot], [("U", tl.tid, c)], ("e2s", si))
            if c == 3:
                hook_mid()
        e2ctr[0] += KC
        hook_p2()

    def main_convB(job, hook_mid, hook_p2):
        ph, tl = job.sub, job.tile
        T = tl.T
        ub = UBs[job.par]
        jj = ph.layer // 2
        for c in range(KC):
            b = c % 2
            w = ph.Wp[c // 4]
            for kc in range(KC):
                op("pe", lambda e, w=w, kc=kc, c=c, b=b: e.matmul(
                    PS[b].ap[:, :T], lhsT=w.ap[:, kc, (c % 4) * 128:(c % 4 + 1) * 128], rhs=HBc[kc].ap[:, :T],
                    start=(kc == 0), stop=(kc == KC - 1)),
                   reads=[w, HBc[kc]], writes=[PS[b]])
            t = CT[b]
            cw = [smallcol(O_CW + (jj * 3 + k) * 8 + c) for k in range(3)]
            op("act", lambda e, t=t, c=c, cw=cw: e.activation(out=t.ap[:, :T], in_=ub.ap[:, c, 0:T], func=AF.Identity,
                                                            scale=cw[0]),
               reads=[ub, SMALL], writes=[t])
            for k in (1, 2):
                op("dve", lambda e, t=t, c=c, k=k, cw=cw: e.scalar_tensor_tensor(
                    out=t.ap[:, :T], in0=ub.ap[:, c, k:k + T], scalar=cw[k], in1=t.ap[:, :T], op0=ALU.mult, op1=ALU.add),
                   reads=[ub, SMALL, t], writes=[t])
            op("dve", lambda e, t=t, c=c, b=b: e.tensor_tensor(out=ZB[c].ap[:, :T], in0=t.ap[:, :T], in1=PS[b].ap[:, :T],
                                                             op=ALU.mult),
               reads=[t, PS[b]], writes=[ZB[c]])
            if c == 3:
                hook_mid()
        hook_p2()
        out_proj(job, lambda k, dc: (ph.Wo[k // 2], ph.Wo[k // 2].ap[:, k % 2, dc * 128:(dc + 1) * 128]), ZB, KC)

    def main_attnA(job, hook_mid, hook_p2):
        ph, tl = job.sub, job.tile
        T = tl.T
        lat = tl.kind == "lat"
        jj = ph.layer // 2
        if lat:
            t0 = tl.g0 - CTX
            dma("sp", COSb.ap, cosd[:, t0:t0 + T], [], [COSb], "cos")
            dma("sp", SINb.ap, sind[:, t0:t0 + T], [], [SINb], "sin")
        for j in range(10):
            b = j % 2
            w = ph.Wp[(j * 128) // 512]
            co = (j * 128) % 512
            for kc in range(KC):
                op("pe", lambda e, w=w, kc=kc, co=co, b=b: e.matmul(
                    PS[b].ap[:, :T], lhsT=w.ap[:, kc, co:co + 128], rhs=HBc[kc].ap[:, :T],
                    start=(kc == 0), stop=(kc == KC - 1)),
                   reads=[w, HBc[kc]], writes=[PS[b]])
            qr = QRAW[b]
            sq = SQs[b]
            op("act", lambda e, qr=qr, b=b: e.activation(out=qr.ap[:, :T], in_=PS[b].ap[:, :T], func=AF.Copy),
               reads=[PS[b]], writes=[qr])
            op("act", lambda e, sq=sq, b=b: e.activation(out=sq.ap[:, :T], in_=PS[b].ap[:, :T], func=AF.Square),
               reads=[PS[b]], writes=[sq])
            op("pe", lambda e, sq=sq: e.matmul(PS[7].ap[:, :T], lhsT=ONESH.ap, rhs=sq.ap[:, :T], start=True, stop=True),
               reads=[sq, ONESH], writes=[PS[7]])
            op("act", lambda e: e.activation(out=RQ.ap[:, :T], in_=PS[7].ap[:, :T], func=AF.Sqrt, bias=EPSC.ap),
               reads=[PS[7], EPSC], writes=[RQ])
            op("dve", lambda e: e.reciprocal(out=RQ.ap[:, :T], in_=RQ.ap[:, :T]), reads=[RQ], writes=[RQ])
            g = smallcol(O_QKG + jj * 2 + (0 if j < NH else 1))
            qn = QN[b]
            op("dve", lambda e, qr=qr, qn=qn, g=g: e.scalar_tensor_tensor(
                out=qn.ap[:, :T], in0=qr.ap[:, :T], scalar=g, in1=RQ.ap[:, :T], op0=ALU.mult, op1=ALU.mult),
               reads=[qr, RQ, SMALL], writes=[qn])
            if j < NH:
                dest_ap = QST.ap[:, j, :T]
                dest_b = QST
            else:
                dest_ap = KT.ap[:, j - NH, tl.g0:tl.g0 + T]
                dest_b = KT
            if lat:
                pp = PS[4 + b]
                op("pe", lambda e, qn=qn, pp=pp: e.matmul(pp.ap[:, :T], lhsT=PERM.ap, rhs=qn.ap[:, :T], start=True, stop=True),
                   reads=[qn, PERM], writes=[pp])
                t1, t2 = SLs[0], SLs[1]
                op("dve", lambda e, qn=qn, t1=t1: e.tensor_tensor(out=t1.ap[:, :T], in0=qn.ap[:, :T], in1=COSb.ap[:, :T],
                                                                op=ALU.mult),
                   reads=[qn, COSb], writes=[t1])
                op("dve", lambda e, pp=pp, t2=t2: e.tensor_tensor(out=t2.ap[:, :T], in0=pp.ap[:, :T], in1=SINb.ap[:, :T],
                                                                op=ALU.mult),
                   reads=[pp, SINb], writes=[t2])
                op("dve", lambda e, t1=t1, t2=t2, dest_ap=dest_ap: e.tensor_tensor(out=dest_ap, in0=t1.ap[:, :T],
                                                                                   in1=t2.ap[:, :T], op=ALU.add),
                   reads=[t1, t2], writes=[dest_b])
            else:
                op("act", lambda e, qn=qn, dest_ap=dest_ap: e.activation(out=dest_ap, in_=qn.ap[:, :T], func=AF.Copy),
                   reads=[qn], writes=[dest_b])
            if j == 4:
                hook_mid()
        wv = ph.Wp[2]
        for st in range(T // 128):
            pv = PS[2 + st % 2]
            for kc in range(KC):
                op("pe", lambda e, kc=kc, st=st, pv=pv: e.matmul(
                    pv.ap[:, 0:256], lhsT=HBc[kc].ap[:, st * 128:(st + 1) * 128], rhs=wv.ap[:, kc, 256:512],
                    start=(kc == 0), stop=(kc == KC - 1)),
                   reads=[HBc[kc], wv], writes=[pv])
            ci = (tl.g0 + st * 128) // 128
            op("act", lambda e, pv=pv, ci=ci: e.activation(out=VV.ap[:, ci, :], in_=pv.ap[:, 0:256], func=AF.Copy),
               reads=[pv], writes=[VV])
        dma("sp", QTd[tl.b, :, :, tl.g0:tl.g0 + T], QST.ap[:, :, :T], [QST], [("QT", tl.tid)], "qts")
        hook_p2()

    def main_attnB(job, hook_mid, hook_p2):
        ph, tl = job.sub, job.tile
        T = tl.T
        nkc = (NTOK // 128) if tl.kind == "lat" else (CTX // 128)
        seq = [(h, c) for h in range(NH) for c in range(nkc)]
        n = len(seq)
        scale = float(HD) ** -0.5

        def S(i):
            h, c = seq[i]
            psb = PS[i % 2]
            op("pe", lambda e: e.matmul(psb.ap[:, :T], lhsT=KT.ap[:, h // 4, c * 128:(c + 1) * 128], rhs=HBf.ap[:, h, :T],
                                        start=True, stop=True),
               reads=[KT, HBf], writes=[psb])

        def EPV(i):
            h, c = seq[i]
            psb = PS[i % 2]
            pb = PBs[i % 4]
            op("act", lambda e: e.activation(out=pb.ap[:, :T], in_=psb.ap[:, :T], func=AF.Exp, scale=scale),
               reads=[psb], writes=[pb])
            ob = PS[2 + h % 2]
            db = PS[7]
            kv = h // 4
            op("pe", lambda e: e.matmul(ob.ap[:, :T], lhsT=VV.ap[:, c, kv * 128:(kv + 1) * 128], rhs=pb.ap[:, :T],
                                        start=(c == 0), stop=(c == nkc - 1)),
               reads=[VV, pb], writes=[ob])
            op("pe", lambda e: e.matmul(db.ap[:, :T], lhsT=ONES.ap, rhs=pb.ap[:, :T], start=(c == 0), stop=(c == nkc - 1)),
               reads=[ONES, pb], writes=[db])
            if c == nkc - 1:
                op("dve", lambda e: e.reciprocal(out=RDEN.ap[:, :T], in_=db.ap[:, :T]), reads=[db], writes=[RDEN])
                op("dve", lambda e: e.tensor_tensor(out=OBc[h].ap[:, :T], in0=ob.ap[:, :T], in1=RDEN.ap[:, :T], op=ALU.mult),
                   reads=[ob, RDEN], writes=[OBc[h]])

        S(0)
        for i in range(n):
            if i + 1 < n:
                S(i + 1)
            EPV(i)
            if i == n // 2:
                hook_mid()
        hook_p2()
        out_proj(job, lambda k, dc: (ph.Wo[k // 2], ph.Wo[k // 2].ap[:, k % 2, dc * 128:(dc + 1) * 128]), OBc, NH)

    MAIN = {"ffn": main_ffn, "final": main_final, "convA": main_convA, "convB": main_convB,
            "attnA": main_attnA, "attnB": main_attnB}

    wl = {}
    for ph in phases:
        wl[id(ph)] = setup_weights(ph) if ph.ty != "final" else ([], [])
    r1, r2 = wl[id(phases[0])]
    for f in r1 + r2:
        f()
    emit_load(jobs[0])
    emit_n1(jobs[0])
    emit_n2(jobs[0])
    for k, job in enumerate(jobs):
        nxt = jobs[k + 1] if k + 1 < len(jobs) else None
        nph = nxt.sub if (nxt is not None and job.last) else None
        late = nxt is not None and (job.sub.ty == "final" or nxt.sub.ty == "attnB")
        if nxt is not None and not late:
            emit_load(nxt)

        def hook_mid(nxt=nxt, late=late):
            if nxt is not None and not late:
                emit_n1(nxt)

        def hook_p2(nxt=nxt, nph=nph, late=late):
            if nph is not None:
                for f in wl[id(nph)][0]:
                    f()
            if nxt is not None:
                if late:
                    emit_load(nxt)
                    emit_n1(nxt)
                emit_n2(nxt)

        MAIN[job.sub.ty](job, hook_mid, hook_p2)
        if nph is not None:
            for f in wl[id(nph)][1]:
                f()
    tr.finish()
    es.close()
    build_program.stats = (tr.ninst, tr.nsem)
    return nc


def _rope_tables():
    t = np.arange(SEQ)
    row = (t // GRID_W).astype(np.float32)
    col = (t % GRID_W).astype(np.float32)
    nf = HD // 4
    inv = (np.float32(10000.0) ** (-(np.arange(nf, dtype=np.float32) / np.float32(nf)))).astype(np.float32)
    ar = (row[None, :] * inv[:, None]).astype(np.float32)
    ac = (col[None, :] * inv[:, None]).astype(np.float32)
    cosT = np.concatenate([np.cos(ar), np.cos(ar), np.cos(ac), np.cos(ac)], 0).astype(np.float32)
    sinT = np.concatenate([-np.sin(ar), np.sin(ar), -np.sin(ac), np.sin(ac)], 0).astype(np.float32)
    perm = np.zeros((128, 128), np.float32)
    for m in range(128):
        perm[m ^ 32, m] = 1.0
    return np.ascontiguousarray(cosT), np.ascontiguousarray(sinT), perm


def _fm(v):
    v = np.asarray(v, np.float32)
    lead = v.shape[:-1]
    return np.moveaxis(v.reshape(lead + (KC, 128)), -1, 0)


def make_in_maps(bpc, x, c, ctx, c_ctx, ada_w, ada_b, norm_g, final_g, ffn_w_in, ffn_w_out, conv_w_in, conv_w,
                 conv_w_out, attn_w_qkv, attn_q_g, attn_k_g, attn_w_o):
    f = np.float32
    cosT, sinT, perm = _rope_tables()
    ada_w_h = np.ascontiguousarray(
        np.asarray(ada_w, f).reshape(DEPTH, KC, 128, 18, 512).transpose(0, 3, 2, 1, 4)).reshape(DEPTH, 18, 128, KC * 512)
    wi = np.asarray(ffn_w_in, f).reshape(DEPTH, 2, KC, 128, 2, FC, 128)
    fwin_h = np.ascontiguousarray(wi.transpose(0, 1, 5, 3, 2, 4, 6)).reshape(DEPTH, 2, FC, 128, KC * 256)
    fwout_h = np.ascontiguousarray(np.asarray(ffn_w_out, f).reshape(DEPTH, 2, FC, 128, D))
    cwin_h = np.ascontiguousarray(np.asarray(conv_w_in, f).reshape(2, KC, 128, 3 * D).transpose(0, 2, 1, 3))
    cwout_h = np.ascontiguousarray(np.asarray(conv_w_out, f).reshape(2, KC, 128, D).transpose(0, 2, 1, 3))
    aqkv_h = np.ascontiguousarray(np.asarray(attn_w_qkv, f).reshape(2, KC, 128, 1536).transpose(0, 2, 1, 3))
    awo_h = np.ascontiguousarray(np.asarray(attn_w_o, f).reshape(2, KC, 128, D).transpose(0, 2, 1, 3))
    adab = np.asarray(ada_b, f).reshape(DEPTH, 72, 128).transpose(2, 0, 1).reshape(128, DEPTH * 72)
    ng = _fm(norm_g).reshape(128, 96)
    fg = _fm(final_g).reshape(128, 8)
    cw = _fm(conv_w).reshape(128, 48)
    qkg = np.stack([np.asarray(attn_q_g, f), np.asarray(attn_k_g, f)], 1).reshape(4, 128).T
    cctx = _fm(c_ctx).reshape(128, 8)
    xf = np.asarray(x, f)
    cf = np.asarray(ctx, f)
    maps = []
    NK = bpc + 1
    NKP = 2 if NK == 2 else 4
    for r in range(NCORES // bpc):
        bs = list(range(r * bpc, (r + 1) * bpc))
        cols = [_fm(np.asarray(c, f)[b]).reshape(128, 8) for b in bs] + [cctx]
        cols += [np.zeros((128, 8), f)] * (NKP - NK)
        cc = np.stack(cols, -1).reshape(128, 8 * NKP)
        small = np.ascontiguousarray(np.concatenate([cc, adab, ng, fg, cw, qkg], 1).astype(f))
        maps.append({
            "xT": np.ascontiguousarray(np.stack([xf[b].T.reshape(KC, 128, SEQ) for b in bs], 0)),
            "ctxT": np.ascontiguousarray(np.stack([cf[b].T.reshape(KC, 128, CTX) for b in bs], 0)),
            "smallp": small, "perm": perm, "cosT": cosT, "sinT": sinT,
            "ada_w": ada_w_h, "ffn_w_in": fwin_h, "ffn_w_out": fwout_h, "conv_w_in": cwin_h,
            "conv_w_out": cwout_h, "attn_w_qkv": aqkv_h, "attn_w_o": awo_h,
        })
    return maps


_NC_CACHE = {}


BPC = 2


def run(inputs, n_sub=12, cores=None, trace=False, bpc=None):
    bpc = BPC if bpc is None else bpc
    if cores is None:
        cores = NCORES // bpc
    key = (n_sub, bpc)
    if key not in _NC_CACHE:
        _NC_CACHE[key] = build_program(n_sub, bpc)
    nc = _NC_CACHE[key]
    maps = make_in_maps(bpc, **inputs)[:cores]
    res = run_bass_kernel_spmd(nc, maps, core_ids=list(range(cores)), trace=trace)
    outs = []
    for r in res.results:
        o = np.asarray(r["outT"]).reshape(bpc, D, SEQ)
        for b in range(bpc):
            outs.append(o[b].T)
    return np.ascontiguousarray(np.stack(outs, 0).astype(np.float32)), res


def kernel(**inputs):
    out, _ = run(inputs)
    return out
```

```python
import numpy as np
from contextlib import ExitStack
import concourse.bass as bass
import concourse.mybir as mybir
from concourse.bass_utils import run_bass_kernel_spmd

F32 = mybir.dt.float32
BF16 = mybir.dt.bfloat16
AF = mybir.ActivationFunctionType
ALU = mybir.AluOpType

D = 1024
KC = 8
DFF = 2816
FC = 22
SEQ = 4096
CTX = 256
NTOK = SEQ + CTX
DEPTH = 4
HD = 128
NH = 8
NKV = 2
GRID_W = 64
EPS = 1e-6
NCORES = 8

C_ONES = 0
C_ONESM = 256
C_ONESH = 512
C_PERM = 768
C_SMALL = 1280
ARENA = 10240
R1 = ARENA
R1_SZ = 90112
R2 = ARENA + R1_SZ
R2_SZ = 45056
XB = R2 + R2_SZ
HB = XB + 16384
ACTB = HB + 8192
SQ = ACTB + 22528
RSTD = SQ + 2048
XR = RSTD + 2048
SL = XR + 4096
E2B = SL + 4096
SB_END = E2B + 3 * 2048
assert SB_END <= 212480
GR = 512

SAME_ENGINE_SYNC = False
SEM_ROLL = 30000


class Buf:
    __slots__ = ("ap", "keys")

    def __init__(self, ap, keys):
        self.ap = ap
        self.keys = keys


def _keys(lst):
    out = []
    for x in lst:
        if isinstance(x, Buf):
            out.extend(x.keys)
        else:
            out.append(x)
    return out


class Trk:
    def __init__(self, nc, es):
        self.nc = nc
        self.es = es
        self.eng = {"pe": nc.tensor, "act": nc.scalar, "dve": nc.vector, "pool": nc.gpsimd, "sp": nc.sync}
        self.esem = {}
        self.ecnt = {}
        self.pe_sems = set()
        self.waited = {e: {} for e in self.eng}
        self.state = {}
        self.dsem = {}
        self.nsem = 0
        self.ninst = 0
        for e in ("pe", "act", "dve", "pool"):
            self._roll(e)

    def _newsem(self, name):
        self.nsem += 1
        return self.es.enter_context(self.nc.semaphore(f"{name}{self.nsem}"))

    def _roll(self, e):
        self.esem[e] = self._newsem(e)
        self.ecnt[e] = 0
        if e == "pe":
            self.pe_sems.add(id(self.esem[e]))

    def _deps(self, rk, wk):
        deps = {}
        st_ = self.state
        for k in rk:
            st = st_.get(k)
            if st is not None and st[0] is not None:
                ts = st[0]
                i = id(ts[0])
                d = deps.get(i)
                if d is None or d[1] < ts[1]:
                    deps[i] = ts
        for k in wk:
            st = st_.get(k)
            if st is not None:
                ts = st[0]
                if ts is not None:
                    i = id(ts[0])
                    d = deps.get(i)
                    if d is None or d[1] < ts[1]:
                        deps[i] = ts
                for i, ts in st[1].items():
                    d = deps.get(i)
                    if d is None or d[1] < ts[1]:
                        deps[i] = ts
        return deps

    def _wait(self, e, deps):
        w = self.waited[e]
        eng = self.eng[e]
        for i, (sem, val) in deps.items():
            if e == "pe" and i in self.pe_sems:
                continue
            if e in ("act", "dve") and (not SAME_ENGINE_SYNC) and sem is self.esem[e]:
                continue
            if w.get(i, 0) < val:
                eng.wait_ge(sem, val)
                w[i] = val

    def _commit(self, ts, rk, wk):
        st_ = self.state
        i = id(ts[0])
        for k in wk:
            st_[k] = [ts, {}]
        for k in rk:
            st = st_.get(k)
            if st is None:
                st = st_[k] = [None, {}]
            st[1][i] = ts

    def op(self, e, fn, reads=(), writes=()):
        rk = _keys(reads)
        wk = _keys(writes)
        self._wait(e, self._deps(rk, wk))
        ins = fn(self.eng[e])
        if self.ecnt[e] >= SEM_ROLL:
            self._roll(e)
        self.ecnt[e] += 1
        ins.then_inc(self.esem[e], 1)
        self._commit((self.esem[e], self.ecnt[e]), rk, wk)
        self.ninst += 1

    def dma(self, e, out, in_, reads, writes, semkey, **kw):
        rk = _keys(reads)
        wk = _keys(writes)
        deps = self._deps(rk, wk)
        ds = self.dsem.get(semkey)
        if ds is None:
            ds = self.dsem[semkey] = [self._newsem("d"), 0]
        if ds[1] > 0:
            deps[id(ds[0])] = (ds[0], ds[1])
        self._wait(e, deps)
        self.eng[e].dma_start(out=out, in_=in_, **kw).then_inc(ds[0], 16)
        ds[1] += 16
        self._commit((ds[0], ds[1]), rk, wk)
        self.ninst += 1

    def finish(self):
        sp = self.eng["sp"]
        for ds in self.dsem.values():
            if ds[1] > 0:
                sp.wait_ge(ds[0], ds[1])


class Tile:
    def __init__(self, b, idx, kind, g0, T, bpc):
        self.b = b
        self.idx = idx
        self.tid = b * 9 + idx
        self.kind = kind
        self.g0 = g0
        self.T = T
        self.k = b if kind == "lat" else bpc


class Job:
    def __init__(self, sub, tile, first, last):
        self.sub = sub
        self.tile = tile
        self.first = first
        self.last = last
        self.par = 0


def build_program(n_sub=12, bpc=1):
    nc = bass.Bass("TRN2", target_bir_lowering=False)
    es = ExitStack()
    NK = bpc + 1
    NKP = 2 if NK == 2 else 4
    NSMALL = 8 * NKP + 288 + 96 + 8 + 48 + 4
    C_SCC = C_SMALL + 4 * NSMALL
    C_MODS = C_SCC + 4 * 8 * NKP
    C_A = C_MODS + 4 * DEPTH * NK * 72
    C_GH = C_A + 4 * 12 * NK * 8
    C_ZERO = C_GH + 4 * 8 * NK * 8
    C_EPS = C_ZERO + 32
    assert C_EPS + 4 <= ARENA
    CTXT = [Tile(b, 0, "ctx", 0, CTX, bpc) for b in range(bpc)]
    LATT = [[Tile(b, 1 + i, "lat", CTX + 512 * i, 512, bpc) for i in range(8)] for b in range(bpc)]

    def din(name, shape, dt=F32):
        return nc.dram_tensor(name, list(shape), dt, kind="ExternalInput").ap()

    xT = din("xT", [bpc, KC, 128, SEQ])
    ctxT = din("ctxT", [bpc, KC, 128, CTX])
    smallp = din("smallp", [128, NSMALL])
    permd = din("perm", [128, 128])
    bsel_d = din("bsel", [10, 10, 128])
    cosd = din("cosT", [128, SEQ])
    sind = din("sinT", [128, SEQ])
    adaw = din("ada_w", [DEPTH, 18, 128, KC * 512])
    fwin = din("ffn_w_in", [DEPTH, 2, FC, 128, KC * 256])
    fwout = din("ffn_w_out", [DEPTH, 2, FC, 128, D])
    cwin = din("conv_w_in", [2, 128, KC, 3 * D])
    cwout = din("conv_w_out", [2, 128, KC, D])
    aqkv = din("attn_w_qkv", [2, 128, KC, 1536])
    awo = din("attn_w_o", [2, 128, KC, D])
    outT = nc.dram_tensor("outT", [bpc, KC, 128, SEQ], F32, kind="ExternalOutput").ap()
    Xd = nc.dram_tensor("Xs", [bpc, KC, 128, NTOK], F32, kind="Internal").ap()
    UW = NTOK + 4
    Ud = nc.dram_tensor("Us", [bpc, KC, 128, UW], F32, kind="Internal").ap()
    QTd = nc.dram_tensor("QTs", [bpc, 128, NH, NTOK], BF16, kind="Internal").ap()

    AR = es.enter_context(nc.sbuf_tensor("AR", [128, SB_END // 4], F32))
    PSt = es.enter_context(nc.psum_tensor("PS", [128, 8, 512], F32))
    tr = Trk(nc, es)
    op = tr.op
    dma = tr.dma

    def sview(off, dims, dt):
        n = int(np.prod(dims))
        esz = 4 if dt is F32 else 2
        nb = n * esz
        assert off % 4 == 0 and nb % 4 == 0 and off + nb <= SB_END, (off, dims)
        ap = AR[:, off // 4:(off + nb) // 4]
        if dt is BF16:
            ap = ap.bitcast(BF16)
        if len(dims) == 2:
            ap = ap.rearrange("p (a b) -> p a b", b=dims[1])
        elif len(dims) == 3:
            ap = ap.rearrange("p (a b c) -> p a b c", b=dims[1], c=dims[2])
        keys = [("S", g) for g in range(off // GR, (off + nb - 1) // GR + 1)]
        return Buf(ap, keys)

    PS = [Buf(PSt[:, b, :], [("P", b)]) for b in range(8)]

    ONES = sview(C_ONES, [128], BF16)
    ONESM = sview(C_ONESM, [128], BF16)
    ONESH = sview(C_ONESH, [128], BF16)
    PERM = sview(C_PERM, [128], F32)
    SMALL = sview(C_SMALL, [NSMALL], F32)
    SCC = sview(C_SCC, [KC, NKP], F32)
    MODS = sview(C_MODS, [DEPTH * NK, 72], F32)
    AV = sview(C_A, [12 * NK, 8], F32)
    GH = sview(C_GH, [8 * NK, 8], F32)
    ZERO = sview(C_ZERO, [8, 1], F32)
    EPSC = sview(C_EPS, [1], F32)

    XBf = sview(XB, [KC, 512], F32)
    XBc = [sview(XB + kc * 2048, [512], F32) for kc in range(KC)]
    HBf = sview(HB, [KC, 512], BF16)
    HBc = [sview(HB + kc * 1024, [512], BF16) for kc in range(KC)]
    ACTc = [sview(ACTB + fc * 1024, [512], BF16) for fc in range(FC)]
    SQs = [sview(SQ + i * 1024, [512], BF16) for i in range(2)]
    RSTDb = sview(RSTD, [512], F32)
    XRs = [sview(XR + i * 2048, [512], F32) for i in range(2)]
    SLs = [sview(SL + i * 2048, [512], F32) for i in range(2)]
    E2s = [sview(E2B + i * 2048, [512], F32) for i in range(3)]
    e2ctr = [0]

    O_CC = 0
    O_ADAB = 8 * NKP
    O_NG = O_ADAB + 288
    O_FG = O_NG + 96
    O_CW = O_FG + 8
    O_QKG = O_CW + 48

    def smallcol(o, n=1):
        return SMALL.ap[:, o:o + n]

    op("pool", lambda e: e.memset(ONES.ap, 1.0), writes=[ONES])
    op("pool", lambda e: e.memset(ONESM.ap, 1.0 / 1024.0), writes=[ONESM])
    op("pool", lambda e: e.memset(ONESH.ap, 1.0 / 128.0), writes=[ONESH])
    op("pool", lambda e: e.memset(ZERO.ap, 0.0), writes=[ZERO])
    op("pool", lambda e: e.memset(EPSC.ap, EPS), writes=[EPSC])
    dma("sp", SMALL.ap, smallp, [], [SMALL], "small")
    dma("sp", PERM.ap, permd, [], [PERM], "perm")
    for b_ in range(bpc):
        for col in (0, CTX + 1, CTX + 2, UW - 1):
            dma("sp", Ud[b_, :, :, col:col + 1].rearrange("k p o -> p k o"), ZERO.ap, [ZERO], [("Upad", col)], "upad",
                allow_slow_non_contiguous=True)
    op("act", lambda e: e.activation(out=SCC.ap.rearrange("p k t -> p (k t)"), in_=smallcol(O_CC, 8 * NKP), func=AF.Silu),
       reads=[SMALL], writes=[SCC])

    STG = [sview(R1 + i * 16384, [KC, 512], F32) for i in range(2)]
    nst = 0
    for i in range(DEPTH):
        for jb in range(18):
            stg = STG[nst % 2]
            dma("sp", stg.ap.rearrange("p k n -> p (k n)"), adaw[i, jb], [], [stg], ("ada", nst % 2))
            nst += 1
            for j4 in range(4):
                j = jb * 4 + j4
                for kc in range(KC):
                    op("pe", lambda e, j=j, j4=j4, kc=kc, stg=stg: e.matmul(
                        PSt[:, 7, NKP * j:NKP * j + NKP], lhsT=stg.ap[:, kc, j4 * 128:(j4 + 1) * 128],
                        rhs=SCC.ap[:, kc, :], start=(kc == 0), stop=(kc == KC - 1)),
                       reads=[stg, SCC], writes=[PS[7]])
        psv = PSt[:, 7, 0:72 * NKP].rearrange("p (j t) -> p j t", t=NKP)
        for kind in range(NK):
            op("dve", lambda e, i=i, kind=kind, psv=psv: e.tensor_tensor(
                out=MODS.ap[:, i * NK + kind, :], in0=psv[:, :, kind], in1=smallcol(O_ADAB + i * 72, 72), op=ALU.add),
               reads=[PS[7], SMALL], writes=[MODS])
        for s in range(3):
            for kind in range(NK):
                m = MODS.ap[:, i * NK + kind, :]
                op("dve", lambda e, m=m, i=i, s=s, kind=kind: e.scalar_tensor_tensor(
                    out=AV.ap[:, (i * 3 + s) * NK + kind, :], in0=m[:, (s * 3 + 1) * 8:(s * 3 + 1) * 8 + 8], scalar=1.0,
                    in1=smallcol(O_NG + (i * 3 + s) * 8, 8), op0=ALU.add, op1=ALU.mult),
                   reads=[MODS, SMALL], writes=[AV])
                if s != 1:
                    jj = 0 if s == 0 else 1
                    op("dve", lambda e, m=m, i=i, s=s, jj=jj, kind=kind: e.tensor_scalar(
                        out=GH.ap[:, (i * 2 + jj) * NK + kind, :], in0=m[:, (s * 3 + 2) * 8:(s * 3 + 2) * 8 + 8],
                        scalar1=0.5, scalar2=None, op0=ALU.mult),
                       reads=[MODS], writes=[GH])

    def mod_A(i, s, kind):
        return AV.ap[:, (i * 3 + s) * NK + kind, :]

    def mod_S(i, s, kind):
        return MODS.ap[:, i * NK + kind, (s * 3) * 8:(s * 3) * 8 + 8]

    def mod_G(i, s, kind):
        if s == 1:
            return MODS.ap[:, i * NK + kind, (s * 3 + 2) * 8:(s * 3 + 2) * 8 + 8]
        return GH.ap[:, (i * 2 + (0 if s == 0 else 1)) * NK + kind, :]

    class Sub:
        pass

    subs = []
    for i in range(DEPTH):
        is_attn = (i % 2) == 1
        last = i == DEPTH - 1
        for s in range(3):
            sb = Sub()
            sb.layer, sb.s = i, s
            if s == 0:
                sb.types = ["ffn"]
                sb.ctx = True
            elif s == 2:
                sb.types = ["ffn"]
                sb.ctx = not last
            elif is_attn:
                sb.types = ["attnA", "attnB"]
                sb.ctx = True
                sb.ctx_out = not last
            else:
                sb.types = ["convA", "convB"]
                sb.ctx = not last
            subs.append(sb)
    subs = subs[:n_sub]

    class Phase:
        pass

    phases = []
    for si, sb in enumerate(subs):
        if sb.types[0] == "attnA":
            plist = [(ty, [b]) for b in range(bpc) for ty in sb.types]
        else:
            plist = [(ty, list(range(bpc))) for ty in sb.types]
        seen = {}
        for ty, bl in plist:
            ph = Phase()
            ph.ty = ty
            ph.sub = sb
            ph.layer, ph.s = sb.layer, sb.s
            ph.first_sub = (si == 0)
            ph.wsrc = seen.get(ty)
            seen.setdefault(ty, ph)
            use_ctx = sb.ctx
            if ty == "attnB":
                use_ctx = sb.ctx_out
            ph.tiles = []
            for b in bl:
                ph.tiles += ([CTXT[b]] if use_ctx else []) + LATT[b]
            phases.append(ph)
    phf = Phase()
    phf.ty = "final"
    phf.layer, phf.s = 0, 0
    phf.first_sub = (len(subs) == 0)
    phf.wsrc = None
    phf.tiles = [t for b in range(bpc) for t in LATT[b]]
    phases.append(phf)

    jobs = []
    for ph in phases:
        n = len(ph.tiles)
        for ti, tl in enumerate(ph.tiles):
            jb = Job(ph, tl, ti == 0, ti == n - 1)
            jb.par = len(jobs) % 2
            jobs.append(jb)

    def src_full(ph, tl):
        if ph.first_sub and ph.ty in ("ffn", "final"):
            if tl.kind == "lat":
                return xT[tl.b, :, :, tl.g0 - CTX:tl.g0 - CTX + tl.T].rearrange("k p t -> p k t"), []
            return ctxT[tl.b, :, :, 0:tl.T].rearrange("k p t -> p k t"), []
        return Xd[tl.b, :, :, tl.g0:tl.g0 + tl.T].rearrange("k p t -> p k t"), [("X", tl.tid, dc) for dc in range(KC)]

    def src_chunk(ph, tl, dc):
        if ph.first_sub and ph.ty == "ffn":
            if tl.kind == "lat":
                return xT[tl.b, dc, :, tl.g0 - CTX:tl.g0 - CTX + tl.T], []
            return ctxT[tl.b, dc, :, 0:tl.T], []
        return Xd[tl.b, dc, :, tl.g0:tl.g0 + tl.T], [("X", tl.tid, dc)]

    def dst_chunk(tl, dc):
        return Xd[tl.b, dc, :, tl.g0:tl.g0 + tl.T], [("X", tl.tid, dc)]

    def ucol(tl):
        return (1 + tl.g0) if tl.kind == "ctx" else (3 + tl.g0)

    def wdma(out_ap, in_ap, buf, idx):
        dma("pool", out_ap, in_ap, [], [buf], ("w", idx))

    def setup_weights(ph):
        r1, r2 = [], []
        ty = ph.ty
        if ph.wsrc is not None:
            for a in ("Win", "Wout", "Wp", "Wo"):
                if hasattr(ph.wsrc, a):
                    setattr(ph, a, getattr(ph.wsrc, a))
            return r1, r2
        if ty == "ffn":
            i, j = ph.layer, (0 if ph.s == 0 else 1)
            ph.Win = [sview(R1 + fc * 4096, [KC, 256], BF16) for fc in range(FC)]
            ph.Wout = [sview(R2 + fp * 4096, [2, D], BF16) for fp in range(FC // 2)]
            for fc in range(FC):
                r1.append(lambda fc=fc: wdma(ph.Win[fc].ap.rearrange("p k n -> p (k n)"), fwin[i, j, fc], ph.Win[fc], fc))
            for fp in range(FC // 2):
                r2.append(lambda fp=fp: wdma(ph.Wout[fp].ap, fwout[i, j, 2 * fp:2 * fp + 2].rearrange("f p n -> p f n"),
                                            ph.Wout[fp], FC + fp))
        elif ty == "convA":
            jj = ph.layer // 2
            ph.Wp = [sview(R1 + pi * 8192, [KC, 512], BF16) for pi in range(4)]
            for pi in range(4):
                r1.append(lambda pi=pi: wdma(ph.Wp[pi].ap, cwin[jj, :, :, D + pi * 512:D + (pi + 1) * 512], ph.Wp[pi], pi))
        elif ty == "convB":
            jj = ph.layer // 2
            ph.Wp = [sview(R1 + pi * 8192, [KC, 512], BF16) for pi in range(2)]
            ph.Wo = [sview(R2 + pi * 4096, [2, D], BF16) for pi in range(4)]
            for pi in range(2):
                r1.append(lambda pi=pi: wdma(ph.Wp[pi].ap, cwin[jj, :, :, pi * 512:(pi + 1) * 512], ph.Wp[pi], pi))
            for pi in range(4):
                r2.append(lambda pi=pi: wdma(ph.Wo[pi].ap, cwout[jj, :, 2 * pi:2 * pi + 2, :], ph.Wo[pi], FC + pi))
        elif ty == "attnA":
            jj = ph.layer // 2
            ph.Wp = [sview(R1 + pi * 8192, [KC, 512], BF16) for pi in range(3)]
            for pi in range(3):
                r1.append(lambda pi=pi: wdma(ph.Wp[pi].ap, aqkv[jj, :, :, pi * 512:(pi + 1) * 512], ph.Wp[pi], pi))
        elif ty == "attnB":
            jj = ph.layer // 2
            ph.Wo = [sview(R2 + pi * 4096, [2, D], BF16) for pi in range(4)]
            for pi in range(4):
                r2.append(lambda pi=pi: wdma(ph.Wo[pi].ap, awo[jj, :, 2 * pi:2 * pi + 2, :], ph.Wo[pi], FC + pi))
        return r1, r2

    KT = sview(R1 + 24576, [NKV, NTOK], BF16)
    VV = sview(R1 + 24576 + 17408, [NTOK // 128, 256], BF16)
    PB2 = [sview(R1 + 59392 + i * 2048, [2, 512], BF16) for i in range(2)]
    OBUF = sview(R2 + 16384, [NH, 512], BF16)
    OBc = [sview(R2 + 16384 + h * 1024, [512], BF16) for h in range(NH)]
    RDEN = sview(R2 + 24576, [512], F32)
    QST = sview(ACTB, [NH, 512], BF16)
    QSTc = [sview(ACTB + h * 1024, [512], BF16) for h in range(NH)]
    QRAW10 = [sview(R1 + 63488 + j * 2048, [512], F32) for j in range(10)]
    SQ10 = [sview(R2 + 26624 + j * 1024, [512], BF16) for j in range(10)]
    SEL = sview(R2 + 36864, [10, 10], BF16)
    BS = sview(R2 + 37376, [10, 128], F32)
    QN = [sview(ACTB + 12288 + i * 2048, [512], F32) for i in range(2)]
    COSb = sview(ACTB + 16384, [512], F32)
    SINb = sview(ACTB + 18432, [512], F32)
    RQ = sview(ACTB + 20480, [512], F32)
    UBs = [sview(R1 + 32768 + i * 16448, [KC, 514], F32) for i in range(2)]
    CT = [sview(R1 + 65664 + i * 2048, [512], F32) for i in range(2)]
    CSL = [sview(R1 + 69760 + i * 2048, [512], F32) for i in range(2)]
    ZB = [sview(R2 + 16384 + kc * 1024, [512], BF16) for kc in range(KC)]

    def emit_load(job):
        ph, tl = job.sub, job.tile
        T = tl.T
        if ph.ty == "attnB":
            dma("sp", HBf.ap[:, :, :T], QTd[tl.b, :, :, tl.g0:tl.g0 + T], [("QT", tl.tid)], [HBf], "qtl")
            return
        ap, keys = src_full(ph, tl)
        dma("sp", XBf.ap[:, :, :T], ap, keys, [XBf], "xl")
        if ph.ty == "convB":
            ub = UBs[job.par]
            c0 = ucol(tl) - 1
            rk = [("U", t2.tid, dc) for t2 in ph.tiles if t2.kind == tl.kind and t2.b == tl.b and abs(t2.tid - tl.tid) <= 1
                  for dc in range(KC)]
            rk += [("Upad", c) for c in (0, CTX + 1, CTX + 2, UW - 1)]
            dma("sp", ub.ap[:, :, :T + 2], Ud[tl.b, :, :, c0:c0 + T + 2].rearrange("k p t -> p k t"), rk, [ub], ("ul", job.par))

    def emit_n1(job):
        ph, tl = job.sub, job.tile
        if ph.ty == "attnB":
            return
        T = tl.T
        for kc in range(KC):
            s = SQs[kc % 2]
            op("act", lambda e, s=s, kc=kc: e.activation(out=s.ap[:, :T], in_=XBc[kc].ap[:, :T], func=AF.Square),
               reads=[XBc[kc]], writes=[s])
            op("pe", lambda e, s=s, kc=kc: e.matmul(PS[6].ap[:, :T], lhsT=ONESM.ap, rhs=s.ap[:, :T],
                                                  start=(kc == 0), stop=(kc == KC - 1)),
               reads=[s, ONESM], writes=[PS[6]])
        op("act", lambda e: e.activation(out=RSTDb.ap[:, :T], in_=PS[6].ap[:, :T], func=AF.Sqrt, bias=EPSC.ap),
           reads=[PS[6], EPSC], writes=[RSTDb])
        op("dve", lambda e: e.reciprocal(out=RSTDb.ap[:, :T], in_=RSTDb.ap[:, :T]), reads=[RSTDb], writes=[RSTDb])

    def emit_n2(job):
        ph, tl = job.sub, job.tile
        if ph.ty in ("attnB", "final"):
            return
        T = tl.T
        A = mod_A(ph.layer, ph.s, tl.k)
        S = mod_S(ph.layer, ph.s, tl.k)
        for kc in range(KC):
            xr = XRs[kc % 2]
            op("dve", lambda e, xr=xr, kc=kc: e.tensor_tensor(out=xr.ap[:, :T], in0=XBc[kc].ap[:, :T], in1=RSTDb.ap[:, :T],
                                                            op=ALU.mult),
               reads=[XBc[kc], RSTDb], writes=[xr])
            op("act", lambda e, xr=xr, kc=kc: e.activation(out=HBc[kc].ap[:, :T], in_=xr.ap[:, :T], func=AF.Identity,
                                                         scale=A[:, kc:kc + 1], bias=S[:, kc:kc + 1]),
               reads=[xr, AV, MODS], writes=[HBc[kc]])

    def e2_reload(job, dc):
        ph, tl = job.sub, job.tile
        slot = E2s[(e2ctr[0] + dc) % 3]
        ap, keys = src_chunk(ph, tl, dc)
        dma("sp", slot.ap[:, :tl.T], ap, keys, [slot], ("e2l", (e2ctr[0] + dc) % 3))

    def e2_update(job, dc, yb):
        ph, tl = job.sub, job.tile
        T = tl.T
        si = (e2ctr[0] + dc) % 3
        slot = E2s[si]
        G = mod_G(ph.layer, ph.s, tl.k)
        op("dve", lambda e: e.scalar_tensor_tensor(out=slot.ap[:, :T], in0=yb.ap[:, :T], scalar=G[:, dc:dc + 1],
                                                   in1=slot.ap[:, :T], op0=ALU.mult, op1=ALU.add),
           reads=[yb, slot, MODS, GH], writes=[slot])
        ap, keys = dst_chunk(tl, dc)
        dma("sp", ap, slot.ap[:, :T], [slot], keys, ("e2s", si))

    def out_proj(job, wfn, rhs_list, nk):
        tl = job.tile
        T = tl.T
        e2_reload(job, 0)
        e2_reload(job, 1)
        for dc in range(KC):
            yb = PS[4 + dc % 2]
            for k in range(nk):
                wb, wap = wfn(k, dc)
                op("pe", lambda e, wap=wap, k=k, yb=yb: e.matmul(yb.ap[:, :T], lhsT=wap, rhs=rhs_list[k].ap[:, :T],
                                                               start=(k == 0), stop=(k == nk - 1)),
                   reads=[wb, rhs_list[k]], writes=[yb])
            e2_update(job, dc, yb)
            if dc + 2 < KC:
                e2_reload(job, dc + 2)
        e2ctr[0] += KC

    def main_ffn(job, hook_mid, hook_p2):
        ph, tl = job.sub, job.tile
        T = tl.T
        for fc in range(FC):
            b = fc % 2
            w = ph.Win[fc]
            for half in range(2):
                psb = PS[2 * half + b]
                for kc in range(KC):
                    op("pe", lambda e, w=w, half=half, kc=kc, psb=psb: e.matmul(
                        psb.ap[:, :T], lhsT=w.ap[:, kc, half * 128:(half + 1) * 128], rhs=HBc[kc].ap[:, :T],
                        start=(kc == 0), stop=(kc == KC - 1)),
                       reads=[w, HBc[kc]], writes=[psb])
            sl = SLs[b]
            op("act", lambda e, sl=sl, b=b: e.activation(out=sl.ap[:, :T], in_=PS[b].ap[:, :T], func=AF.Silu),
               reads=[PS[b]], writes=[sl])
            op("dve", lambda e, sl=sl, b=b, fc=fc: e.tensor_tensor(out=ACTc[fc].ap[:, :T], in0=sl.ap[:, :T],
                                                                 in1=PS[2 + b].ap[:, :T], op=ALU.mult),
               reads=[sl, PS[2 + b]], writes=[ACTc[fc]])
            if fc == 6:
                hook_mid()
        hook_p2()
        out_proj(job, lambda k, dc: (ph.Wout[k // 2], ph.Wout[k // 2].ap[:, k % 2, dc * 128:(dc + 1) * 128]), ACTc, FC)

    def main_final(job, hook_mid, hook_p2):
        tl = job.tile
        T = tl.T
        t0 = tl.g0 - CTX
        for dc in range(KC):
            si = (e2ctr[0] + dc) % 3
            slot = E2s[si]
            op("dve", lambda e, dc=dc, slot=slot: e.scalar_tensor_tensor(
                out=slot.ap[:, :T], in0=XBc[dc].ap[:, :T], scalar=smallcol(O_FG + dc), in1=RSTDb.ap[:, :T],
                op0=ALU.mult, op1=ALU.mult),
               reads=[XBc[dc], SMALL, RSTDb], writes=[slot])
            dma("sp", outT[tl.b, dc, :, t0:t0 + T], slot.ap[:, :T], [slot], [("OUT", tl.tid, dc)], ("e2s", si))
        e2ctr[0] += KC
        hook_mid()
        hook_p2()

    def main_convA(job, hook_mid, hook_p2):
        ph, tl = job.sub, job.tile
        T = tl.T
        c0 = ucol(tl)
        for c in range(KC):
            b = c % 2
            for half in range(2):
                w = ph.Wp[2 * half + c // 4]
                psb = PS[2 * half + b]
                for kc in range(KC):
                    op("pe", lambda e, w=w, kc=kc, psb=psb, c=c: e.matmul(
                        psb.ap[:, :T], lhsT=w.ap[:, kc, (c % 4) * 128:(c % 4 + 1) * 128], rhs=HBc[kc].ap[:, :T],
                        start=(kc == 0), stop=(kc == KC - 1)),
                       reads=[w, HBc[kc]], writes=[psb])
            cs = CSL[b]
            op("act", lambda e, cs=cs, b=b: e.activation(out=cs.ap[:, :T], in_=PS[b].ap[:, :T], func=AF.Copy),
               reads=[PS[b]], writes=[cs])
            si = (e2ctr[0] + c) % 3
            slot = E2s[si]
            op("dve", lambda e, cs=cs, b=b, slot=slot: e.tensor_tensor(out=slot.ap[:, :T], in0=cs.ap[:, :T],
                                                                     in1=PS[2 + b].ap[:, :T], op=ALU.mult),
               reads=[cs, PS[2 + b]], writes=[slot])
            dma("sp", Ud[tl.b, c, :, c0:c0 + T], slot.ap[:, :T], [slot], [("U", tl.tid, c)], ("e2s", si))
            if c == 3:
                hook_mid()
        e2ctr[0] += KC
        hook_p2()

    def main_convB(job, hook_mid, hook_p2):
        ph, tl = job.sub, job.tile
        T = tl.T
        ub = UBs[job.par]
        jj = ph.layer // 2
        for c in range(KC):
            b = c % 2
            w = ph.Wp[c // 4]
            for kc in range(KC):
                op("pe", lambda e, w=w, kc=kc, c=c, b=b: e.matmul(
                    PS[b].ap[:, :T], lhsT=w.ap[:, kc, (c % 4) * 128:(c % 4 + 1) * 128], rhs=HBc[kc].ap[:, :T],
                    start=(kc == 0), stop=(kc == KC - 1)),
                   reads=[w, HBc[kc]], writes=[PS[b]])
            t = CT[b]
            cw = [smallcol(O_CW + (jj * 3 + k) * 8 + c) for k in range(3)]
            op("act", lambda e, t=t, c=c, cw=cw: e.activation(out=t.ap[:, :T], in_=ub.ap[:, c, 0:T], func=AF.Identity,
                                                            scale=cw[0]),
               reads=[ub, SMALL], writes=[t])
            for k in (1, 2):
                op("dve", lambda e, t=t, c=c, k=k, cw=cw: e.scalar_tensor_tensor(
                    out=t.ap[:, :T], in0=ub.ap[:, c, k:k + T], scalar=cw[k], in1=t.ap[:, :T], op0=ALU.mult, op1=ALU.add),
                   reads=[ub, SMALL, t], writes=[t])
            op("dve", lambda e, t=t, c=c, b=b: e.tensor_tensor(out=ZB[c].ap[:, :T], in0=t.ap[:, :T], in1=PS[b].ap[:, :T],
                                                             op=ALU.mult),
               reads=[t, PS[b]], writes=[ZB[c]])
            if c == 3:
                hook_mid()
        hook_p2()
        out_proj(job, lambda k, dc: (ph.Wo[k // 2], ph.Wo[k // 2].ap[:, k % 2, dc * 128:(dc + 1) * 128]), ZB, KC)

    def main_attnA(job, hook_mid, hook_p2):
        ph, tl = job.sub, job.tile
        T = tl.T
        lat = tl.kind == "lat"
        jj = ph.layer // 2
        if job.first:
            op("pool", lambda e: e.memset(SEL.ap, 0.0), writes=[SEL])
            op("pool", lambda e: e.memset(BS.ap, 0.0), writes=[BS])
            for j in range(10):
                op("pool", lambda e, j=j: e.memset(SEL.ap[:, j, j:j + 1], 1.0 / 128.0), writes=[SEL])
            dma("sp", BS.ap[0:10, :, :], bsel_d, [], [BS], "bsel")
        if lat:
            t0 = tl.g0 - CTX
            dma("sp", COSb.ap, cosd[:, t0:t0 + T], [], [COSb], "cos")
            dma("sp", SINb.ap, sind[:, t0:t0 + T], [], [SINb], "sin")
        for j in range(10):
            b = j % 2
            w = ph.Wp[(j * 128) // 512]
            co = (j * 128) % 512
            for kc in range(KC):
                op("pe", lambda e, w=w, kc=kc, co=co, b=b: e.matmul(
                    PS[b].ap[:, :T], lhsT=w.ap[:, kc, co:co + 128], rhs=HBc[kc].ap[:, :T],
                    start=(kc == 0), stop=(kc == KC - 1)),
                   reads=[w, HBc[kc]], writes=[PS[b]])
            qr = QRAW10[j]
            sq = SQ10[j]
            op("act", lambda e, qr=qr, b=b: e.activation(out=qr.ap[:, :T], in_=PS[b].ap[:, :T], func=AF.Copy),
               reads=[PS[b]], writes=[qr])
            op("act", lambda e, sq=sq, b=b: e.activation(out=sq.ap[:, :T], in_=PS[b].ap[:, :T], func=AF.Square),
               reads=[PS[b]], writes=[sq])
            op("pe", lambda e, sq=sq, j=j: e.matmul(PS[7].ap[0:10, :T], lhsT=SEL.ap[:, j, :], rhs=sq.ap[:, :T],
                                                  start=(j == 0), stop=(j == 9)),
               reads=[sq, SEL], writes=[PS[7]])
            if j == 4:
                hook_mid()
        op("act", lambda e: e.activation(out=RQ.ap[0:10, :T], in_=PS[7].ap[0:10, :T], func=AF.Sqrt, bias=EPSC.ap[0:10, :]),
           reads=[PS[7], EPSC], writes=[RQ])
        op("dve", lambda e: e.reciprocal(out=RQ.ap[0:10, :T], in_=RQ.ap[0:10, :T]), reads=[RQ], writes=[RQ])
        for j in range(10):
            b = j % 2
            bc = PS[2 + b]
            op("pe", lambda e, j=j, bc=bc: e.matmul(bc.ap[:, :T], lhsT=BS.ap[0:10, j, :], rhs=RQ.ap[0:10, :T],
                                                  start=True, stop=True),
               reads=[BS, RQ], writes=[bc])
            g = smallcol(O_QKG + jj * 2 + (0 if j < NH else 1))
            qr = QRAW10[j]
            qn = QN[b]
            op("dve", lambda e, qr=qr, qn=qn, g=g, bc=bc: e.scalar_tensor_tensor(
                out=qn.ap[:, :T], in0=qr.ap[:, :T], scalar=g, in1=bc.ap[:, :T], op0=ALU.mult, op1=ALU.mult),
               reads=[qr, bc, SMALL], writes=[qn])
            if j < NH:
                dest_ap = QST.ap[:, j, :T]
                dest_b = QSTc[j]
            else:
                dest_ap = KT.ap[:, j - NH, tl.g0:tl.g0 + T]
                dest_b = KT
            if lat:
                pp = PS[4 + b]
                op("pe", lambda e, qn=qn, pp=pp: e.matmul(pp.ap[:, :T], lhsT=PERM.ap, rhs=qn.ap[:, :T], start=True, stop=True),
                   reads=[qn, PERM], writes=[pp])
                t1, t2 = SLs[0], SLs[1]
                op("dve", lambda e, qn=qn, t1=t1: e.tensor_tensor(out=t1.ap[:, :T], in0=qn.ap[:, :T], in1=COSb.ap[:, :T],
                                                                op=ALU.mult),
                   reads=[qn, COSb], writes=[t1])
                op("dve", lambda e, pp=pp, t2=t2: e.tensor_tensor(out=t2.ap[:, :T], in0=pp.ap[:, :T], in1=SINb.ap[:, :T],
                                                                op=ALU.mult),
                   reads=[pp, SINb], writes=[t2])
                op("dve", lambda e, t1=t1, t2=t2, dest_ap=dest_ap: e.tensor_tensor(out=dest_ap, in0=t1.ap[:, :T],
                                                                                  in1=t2.ap[:, :T], op=ALU.add),
                   reads=[t1, t2], writes=[dest_b])
            else:
                op("act", lambda e, qn=qn, dest_ap=dest_ap: e.activation(out=dest_ap, in_=qn.ap[:, :T], func=AF.Copy),
                   reads=[qn], writes=[dest_b])
        wv = ph.Wp[2]
        for st in range(T // 128):
            pv = PS[st % 2]
            for kc in range(KC):
                op("pe", lambda e, kc=kc, st=st, pv=pv: e.matmul(
                    pv.ap[:, 0:256], lhsT=HBc[kc].ap[:, st * 128:(st + 1) * 128], rhs=wv.ap[:, kc, 256:512],
                    start=(kc == 0), stop=(kc == KC - 1)),
                   reads=[HBc[kc], wv], writes=[pv])
            ci = (tl.g0 + st * 128) // 128
            op("act", lambda e, pv=pv, ci=ci: e.activation(out=VV.ap[:, ci, :], in_=pv.ap[:, 0:256], func=AF.Copy),
               reads=[pv], writes=[VV])
        dma("sp", QTd[tl.b, :, :, tl.g0:tl.g0 + T], QST.ap[:, :, :T], [QST], [("QT", tl.tid)], "qts")
        hook_p2()

    def main_attnB(job, hook_mid, hook_p2):
        ph, tl = job.sub, job.tile
        T = tl.T
        nkc = (NTOK // 128) if tl.kind == "lat" else (CTX // 128)
        npair = nkc // 2
        seq = [(h, p) for h in range(NH) for p in range(npair)]
        n = len(seq)
        scale = float(HD) ** -0.5

        def S(i):
            h, p = seq[i]
            base = 2 * (i % 2)
            for u in range(2):
                c = 2 * p + u
                psb = PS[base + u]
                op("pe", lambda e, psb=psb, c=c: e.matmul(psb.ap[:, :T], lhsT=KT.ap[:, h // 4, c * 128:(c + 1) * 128],
                                                        rhs=HBf.ap[:, h, :T], start=True, stop=True),
                   reads=[KT, HBf], writes=[psb])

        def EPV(i):
            h, p = seq[i]
            base = 2 * (i % 2)
            pb = PB2[i % 2]
            op("act", lambda e: e.activation(out=pb.ap[:, :, :T], in_=PSt[:, base:base + 2, :T], func=AF.Exp, scale=scale),
               reads=[PS[base], PS[base + 1]], writes=[pb])
            ob = PS[4 + h % 2]
            db = PS[7]
            kv = h // 4
            for u in range(2):
                c = 2 * p + u
                op("pe", lambda e, c=c, u=u: e.matmul(ob.ap[:, :T], lhsT=VV.ap[:, c, kv * 128:(kv + 1) * 128],
                                                      rhs=pb.ap[:, u, :T], start=(c == 0), stop=(c == nkc - 1)),
                   reads=[VV, pb], writes=[ob])
                op("pe", lambda e, c=c, u=u: e.matmul(db.ap[:, :T], lhsT=ONES.ap, rhs=pb.ap[:, u, :T],
                                                      start=(c == 0), stop=(c == nkc - 1)),
                   reads=[ONES, pb], writes=[db])
            if p == npair - 1:
                op("dve", lambda e: e.reciprocal(out=RDEN.ap[:, :T], in_=db.ap[:, :T]), reads=[db], writes=[RDEN])
                op("dve", lambda e: e.tensor_tensor(out=OBc[h].ap[:, :T], in0=ob.ap[:, :T], in1=RDEN.ap[:, :T], op=ALU.mult),
                   reads=[ob, RDEN], writes=[OBc[h]])

        S(0)
        for i in range(n):
            if i + 1 < n:
                S(i + 1)
            EPV(i)
            if i == n // 2:
                hook_mid()
        hook_p2()
        out_proj(job, lambda k, dc: (ph.Wo[k // 2], ph.Wo[k // 2].ap[:, k % 2, dc * 128:(dc + 1) * 128]), OBc, NH)

    MAIN = {"ffn": main_ffn, "final": main_final, "convA": main_convA, "convB": main_convB,
            "attnA": main_attnA, "attnB": main_attnB}

    wl = {}
    for ph in phases:
        wl[id(ph)] = setup_weights(ph) if ph.ty != "final" else ([], [])
    r1, r2 = wl[id(phases[0])]
    for f in r1 + r2:
        f()
    emit_load(jobs[0])
    emit_n1(jobs[0])
    emit_n2(jobs[0])
    for k, job in enumerate(jobs):
        nxt = jobs[k + 1] if k + 1 < len(jobs) else None
        nph = nxt.sub if (nxt is not None and job.last) else None
        late = nxt is not None and (job.sub.ty == "final" or nxt.sub.ty == "attnB")
        if nxt is not None and not late:
            emit_load(nxt)

        def hook_mid(nxt=nxt, late=late):
            if nxt is not None and not late:
                emit_n1(nxt)

        def hook_p2(nxt=nxt, nph=nph, late=late):
            if nph is not None:
                for f in wl[id(nph)][0]:
                    f()
            if nxt is not None:
                if late:
                    emit_load(nxt)
                    emit_n1(nxt)
                emit_n2(nxt)

        MAIN[job.sub.ty](job, hook_mid, hook_p2)
        if nph is not None:
            for f in wl[id(nph)][1]:
                f()
    tr.finish()
    es.close()
    build_program.stats = (tr.ninst, tr.nsem)
    return nc


def _rope_tables():
    t = np.arange(SEQ)
    row = (t // GRID_W).astype(np.float32)
    col = (t % GRID_W).astype(np.float32)
    nf = HD // 4
    inv = (np.float32(10000.0) ** (-(np.arange(nf, dtype=np.float32) / np.float32(nf)))).astype(np.float32)
    ar = (row[None, :] * inv[:, None]).astype(np.float32)
    ac = (col[None, :] * inv[:, None]).astype(np.float32)
    cosT = np.concatenate([np.cos(ar), np.cos(ar), np.cos(ac), np.cos(ac)], 0).astype(np.float32)
    sinT = np.concatenate([-np.sin(ar), np.sin(ar), -np.sin(ac), np.sin(ac)], 0).astype(np.float32)
    perm = np.zeros((128, 128), np.float32)
    for m in range(128):
        perm[m ^ 32, m] = 1.0
    return np.ascontiguousarray(cosT), np.ascontiguousarray(sinT), perm


def _fm(v):
    v = np.asarray(v, np.float32)
    lead = v.shape[:-1]
    return np.moveaxis(v.reshape(lead + (KC, 128)), -1, 0)


def make_in_maps(bpc, x, c, ctx, c_ctx, ada_w, ada_b, norm_g, final_g, ffn_w_in, ffn_w_out, conv_w_in, conv_w,
                 conv_w_out, attn_w_qkv, attn_q_g, attn_k_g, attn_w_o):
    f = np.float32
    cosT, sinT, perm = _rope_tables()
    bsel = np.zeros((10, 10, 128), f)
    for j in range(10):
        bsel[j, j, :] = 1.0
    ada_w_h = np.ascontiguousarray(
        np.asarray(ada_w, f).reshape(DEPTH, KC, 128, 18, 512).transpose(0, 3, 2, 1, 4)).reshape(DEPTH, 18, 128, KC * 512)
    wi = np.asarray(ffn_w_in, f).reshape(DEPTH, 2, KC, 128, 2, FC, 128)
    fwin_h = np.ascontiguousarray(wi.transpose(0, 1, 5, 3, 2, 4, 6)).reshape(DEPTH, 2, FC, 128, KC * 256)
    fwout_h = np.ascontiguousarray(np.asarray(ffn_w_out, f).reshape(DEPTH, 2, FC, 128, D))
    cwin_h = np.ascontiguousarray(np.asarray(conv_w_in, f).reshape(2, KC, 128, 3 * D).transpose(0, 2, 1, 3))
    cwout_h = np.ascontiguousarray(np.asarray(conv_w_out, f).reshape(2, KC, 128, D).transpose(0, 2, 1, 3))
    aqkv_h = np.ascontiguousarray(np.asarray(attn_w_qkv, f).reshape(2, KC, 128, 1536).transpose(0, 2, 1, 3))
    awo_h = np.ascontiguousarray(np.asarray(attn_w_o, f).reshape(2, KC, 128, D).transpose(0, 2, 1, 3))
    adab = np.asarray(ada_b, f).reshape(DEPTH, 72, 128).transpose(2, 0, 1).reshape(128, DEPTH * 72)
    ng = _fm(norm_g).reshape(128, 96)
    fg = _fm(final_g).reshape(128, 8)
    cw = _fm(conv_w).reshape(128, 48)
    qkg = np.stack([np.asarray(attn_q_g, f), np.asarray(attn_k_g, f)], 1).reshape(4, 128).T
    cctx = _fm(c_ctx).reshape(128, 8)
    xf = np.asarray(x, f)
    cf = np.asarray(ctx, f)
    maps = []
    NK = bpc + 1
    NKP = 2 if NK == 2 else 4
    for r in range(NCORES // bpc):
        bs = list(range(r * bpc, (r + 1) * bpc))
        cols = [_fm(np.asarray(c, f)[b]).reshape(128, 8) for b in bs] + [cctx]
        cols += [np.zeros((128, 8), f)] * (NKP - NK)
        cc = np.stack(cols, -1).reshape(128, 8 * NKP)
        small = np.ascontiguousarray(np.concatenate([cc, adab, ng, fg, cw, qkg], 1).astype(f))
        maps.append({
            "xT": np.ascontiguousarray(np.stack([xf[b].T.reshape(KC, 128, SEQ) for b in bs], 0)),
            "ctxT": np.ascontiguousarray(np.stack([cf[b].T.reshape(KC, 128, CTX) for b in bs], 0)),
            "smallp": small, "perm": perm, "cosT": cosT, "sinT": sinT, "bsel": bsel,
            "ada_w": ada_w_h, "ffn_w_in": fwin_h, "ffn_w_out": fwout_h, "conv_w_in": cwin_h,
            "conv_w_out": cwout_h, "attn_w_qkv": aqkv_h, "attn_w_o": awo_h,
        })
    return maps


_NC_CACHE = {}


BPC = 2


def run(inputs, n_sub=12, cores=None, trace=False, bpc=None):
    bpc = BPC if bpc is None else bpc
    if cores is None:
        cores = NCORES // bpc
    key = (n_sub, bpc)
    if key not in _NC_CACHE:
        _NC_CACHE[key] = build_program(n_sub, bpc)
    nc = _NC_CACHE[key]
    maps = make_in_maps(bpc, **inputs)[:cores]
    res = run_bass_kernel_spmd(nc, maps, core_ids=list(range(cores)), trace=trace)
    outs = []
    for r in res.results:
        o = np.asarray(r["outT"]).reshape(bpc, D, SEQ)
        for b in range(bpc):
            outs.append(o[b].T)
    return np.ascontiguousarray(np.stack(outs, 0).astype(np.float32)), res


def kernel(**inputs):
    out, _ = run(inputs)
    return out
```
